# Optimizing a Trainium2 kernel written in Bass

```python
import jax, jax.numpy as jnp
from jax import lax
import numpy as np

D_MODEL = 1024
BATCH = 4
SEQ = 4096
DEPTH = 1

MEM_LEN = 256
HG_HEADS = 4
HG_DK = 128
HG_DV = 128
HG_WIDTH = HG_HEADS * HG_DV
HG_CHUNK = 64
SG_GROUPS = 4
SG_DIM = 128
SG_WIDTH = SG_GROUPS * SG_DIM
SG_CHUNK = 128
MIX_WIDTH = HG_WIDTH + SG_WIDTH
IN_WIDTH = 4 * HG_WIDTH + 2 * SG_WIDTH
X_HEADS = 4
X_HEAD_DIM = D_MODEL // X_HEADS
D_FF = 2816
ALPHA = (2.0 * DEPTH) ** 0.25
BETA = (8.0 * DEPTH) ** -0.25
LN_EPS = 1e-5

kernel_name = "hybrid_hgrn2_sgu_macaron_deepnorm"


def _layer_norm(x, g, b):
    xf = x.astype(jnp.float32)
    mu = jnp.mean(xf, axis=-1, keepdims=True)
    var = jnp.mean(jnp.square(xf - mu), axis=-1, keepdims=True)
    return ((xf - mu) * lax.rsqrt(var + LN_EPS) * g + b).astype(x.dtype)


def _rms_norm(x, g):
    xf = x.astype(jnp.float32)
    return xf * lax.rsqrt(jnp.mean(jnp.square(xf), axis=-1, keepdims=True) + LN_EPS) * g


def _swiglu(x, w_gate, w_up, w_down):
    return (jax.nn.silu(x @ w_gate) * (x @ w_up)) @ w_down


def _hgrn2(q, fz, iv, lb):
    B, T, H, DK = q.shape
    DV = iv.shape[-1]
    n_chunks = T // HG_CHUNK
    q = q.astype(jnp.float32)
    fz = fz.astype(jnp.float32)
    iv = iv.astype(jnp.float32)
    log_f = jnp.log(lb + (1.0 - lb) * jax.nn.sigmoid(fz))
    k = (1.0 - lb) * jax.nn.sigmoid(-fz)

    def chunks(a):
        return a.reshape(B, n_chunks, HG_CHUNK, H, a.shape[-1]).transpose(1, 0, 3, 2, 4)

    causal = jnp.tril(jnp.ones((HG_CHUNK, HG_CHUNK), dtype=bool))[:, :, None]

    def step(S, inp):
        qc, kc, vc, lfc = inp
        b = jnp.cumsum(lfc, axis=2)
        diff = b[:, :, :, None, :] - b[:, :, None, :, :]
        decay = jnp.where(causal, jnp.exp(jnp.where(causal, diff, 0.0)), 0.0)
        scores = jnp.einsum('bhtd,bhsd,bhtsd->bhts', qc, kc, decay)
        o = (jnp.einsum('bhts,bhsv->bhtv', scores, vc)
             + jnp.einsum('bhtd,bhdv->bhtv', qc * jnp.exp(b), S))
        b_end = b[:, :, -1:, :]
        S = (jnp.exp(b_end[:, :, 0, :])[..., None] * S
             + jnp.einsum('bhsd,bhsv->bhdv', kc * jnp.exp(b_end - b), vc))
        return S, o

    S0 = jnp.zeros((B, H, DK, DV), jnp.float32)
    _, o = lax.scan(step, S0, (chunks(q), chunks(k), chunks(iv), chunks(log_f)))
    return o.transpose(1, 0, 3, 2, 4).reshape(B, T, H, DV)


def _spatial_gating(uv, w_s, b_s, ln_g, ln_b):
    B, T, _ = uv.shape
    u, v = jnp.split(uv, 2, axis=-1)
    v = _layer_norm(v.reshape(B, T, SG_GROUPS, SG_DIM), ln_g, ln_b)
    v = v.reshape(B, T // SG_CHUNK, SG_CHUNK, SG_GROUPS, SG_DIM)
    causal = jnp.tril(jnp.ones((SG_CHUNK, SG_CHUNK), dtype=bool))
    w = jnp.where(causal, w_s, 0.0)
    s = jnp.einsum('gts,bnsgc->bntgc', w, v) + b_s.T[None, None, :, :, None]
    return u * s.reshape(B, T, SG_WIDTH)


def _token_mixers(h, w_in, lb, hg_norm_g, sg_ln_g, sg_ln_b, sg_w_s, sg_b_s, w_out):
    B, T, _ = h.shape
    proj = h @ w_in
    q, fz, iv, g, uv = jnp.split(
        proj, [HG_WIDTH, 2 * HG_WIDTH, 3 * HG_WIDTH, 4 * HG_WIDTH], axis=-1)
    heads = lambda a: a.reshape(B, T, HG_HEADS, -1)
    o = _hgrn2(heads(q), heads(fz), heads(iv), lb)
    o = _rms_norm(o, hg_norm_g) * jax.nn.silu(heads(g).astype(jnp.float32))
    o_a = o.reshape(B, T, HG_WIDTH).astype(h.dtype)
    o_b = _spatial_gating(jax.nn.gelu(uv), sg_w_s, sg_b_s, sg_ln_g, sg_ln_b)
    return jnp.concatenate([o_a, o_b], axis=-1) @ w_out


def _memory_cross_attention(h, mem, mem_g, mem_b, wq, wk, wv, wo):
    B, T, _ = h.shape
    M = mem.shape[1]
    m = _layer_norm(mem, mem_g, mem_b)
    q = (h @ wq).reshape(B, T, X_HEADS, X_HEAD_DIM)
    k = (m @ wk).reshape(B, M, X_HEADS, X_HEAD_DIM)
    v = (m @ wv).reshape(B, M, X_HEADS, X_HEAD_DIM)
    s = jnp.einsum('bthd,bmhd->bhtm', q.astype(jnp.float32), k.astype(jnp.float32)) * (X_HEAD_DIM ** -0.5)
    p = jax.nn.softmax(s, axis=-1).astype(h.dtype)
    o = jnp.einsum('bhtm,bmhd->bthd', p, v).reshape(B, T, D_MODEL)
    return o @ wo


def setup_inputs(seed: int = 0) -> dict:
    key = jax.random.key(seed)
    ks = iter(jax.random.split(key, 48))
    L = DEPTH

    def nrm(shape, scale):
        return jax.random.normal(next(ks), shape, jnp.float32) * scale

    def gain(shape):
        return 1.0 + nrm(shape, 0.05)

    def bias(shape):
        return nrm(shape, 0.01)

    d_in = D_MODEL ** -0.5
    f_in = D_FF ** -0.5
    return {
        "x": nrm((BATCH, SEQ, D_MODEL), 1.0),
        "mem": nrm((BATCH, MEM_LEN, D_MODEL), 1.0),
        "ffn1_w_gate": nrm((L, D_MODEL, D_FF), d_in),
        "ffn1_w_up": nrm((L, D_MODEL, D_FF), d_in),
        "ffn1_w_down": nrm((L, D_FF, D_MODEL), f_in * BETA),
        "ln1_g": gain((L, D_MODEL)),
        "ln1_b": bias((L, D_MODEL)),
        "w_in": nrm((L, D_MODEL, IN_WIDTH), d_in),
        "hg_lb_logits": nrm((DEPTH + 1, HG_HEADS, HG_DK), 0.5),
        "hg_norm_g": gain((L, HG_DV)),
        "sg_ln_g": gain((L, SG_GROUPS, SG_DIM)),
        "sg_ln_b": bias((L, SG_GROUPS, SG_DIM)),
        "sg_w_s": nrm((L, SG_GROUPS, SG_CHUNK, SG_CHUNK), SG_CHUNK ** -0.5),
        "sg_b_s": gain((L, SG_GROUPS, SG_CHUNK)),
        "w_out": nrm((L, MIX_WIDTH, D_MODEL), (MIX_WIDTH ** -0.5) * BETA),
        "ln2_g": gain((L, D_MODEL)),
        "ln2_b": bias((L, D_MODEL)),
        "mem_ln_g": gain((L, D_MODEL)),
        "mem_ln_b": bias((L, D_MODEL)),
        "xa_w_q": nrm((L, D_MODEL, D_MODEL), d_in),
        "xa_w_k": nrm((L, D_MODEL, D_MODEL), d_in),
        "xa_w_v": nrm((L, D_MODEL, D_MODEL), d_in * BETA),
        "xa_w_o": nrm((L, D_MODEL, D_MODEL), d_in * BETA),
        "ln3_g": gain((L, D_MODEL)),
        "ln3_b": bias((L, D_MODEL)),
        "ffn2_w_gate": nrm((L, D_MODEL, D_FF), d_in),
        "ffn2_w_up": nrm((L, D_MODEL, D_FF), d_in),
        "ffn2_w_down": nrm((L, D_FF, D_MODEL), f_in * BETA),
        "ln4_g": gain((L, D_MODEL)),
        "ln4_b": bias((L, D_MODEL)),
    }


def reference(x, mem, ffn1_w_gate, ffn1_w_up, ffn1_w_down, ln1_g, ln1_b,
              w_in, hg_lb_logits, hg_norm_g, sg_ln_g, sg_ln_b, sg_w_s, sg_b_s,
              w_out, ln2_g, ln2_b, mem_ln_g, mem_ln_b, xa_w_q, xa_w_k, xa_w_v,
              xa_w_o, ln3_g, ln3_b, ffn2_w_gate, ffn2_w_up, ffn2_w_down,
              ln4_g, ln4_b):
    lower_bounds = jnp.cumsum(jax.nn.softmax(hg_lb_logits.astype(jnp.float32), axis=0), axis=0)
    h = x
    for l in range(DEPTH):
        h = _layer_norm(ALPHA * h + 0.5 * _swiglu(h, ffn1_w_gate[l], ffn1_w_up[l], ffn1_w_down[l]),
                        ln1_g[l], ln1_b[l])
        mix = _token_mixers(h, w_in[l], lower_bounds[l], hg_norm_g[l], sg_ln_g[l], sg_ln_b[l],
                            sg_w_s[l], sg_b_s[l], w_out[l])
        h = _layer_norm(ALPHA * h + mix, ln2_g[l], ln2_b[l])
        xa = _memory_cross_attention(h, mem, mem_ln_g[l], mem_ln_b[l], xa_w_q[l], xa_w_k[l],
                                     xa_w_v[l], xa_w_o[l])
        h = _layer_norm(ALPHA * h + xa, ln3_g[l], ln3_b[l])
        h = _layer_norm(ALPHA * h + 0.5 * _swiglu(h, ffn2_w_gate[l], ffn2_w_up[l], ffn2_w_down[l]),
                        ln4_g[l], ln4_b[l])
    return h
```

```python
import numpy as np
from contextlib import ExitStack
import concourse.bass as bass
import concourse.mybir as mybir
from concourse.bass_utils import run_bass_kernel_spmd

F32 = mybir.dt.float32
BF16 = mybir.dt.bfloat16
AF = mybir.ActivationFunctionType
ALU = mybir.AluOpType

NCORES = 8
P = 128
D = 1024
KC = 8
DFF = 2816
FC = 22
NFB = 11
T = 2048
GS = 512
NG = 4
MEM = 256
ALPHA = 2.0 ** 0.25
EPS = 1e-5
STAGE = 4
USE_WCACHE = False

C_LN = {1: (0, 8), 2: (16, 24), 3: (32, 40), 4: (48, 56), "m": (64, 72)}
C_LB0, C_LB1, C_HGN, NCOLS = 80, 84, 88, 89

ENGS = ("pe", "act", "dve", "pool", "sp")


class Sched:
    def __init__(self, nc, stack):
        self.nc = nc
        self.stack = stack
        self.streams = {e: [] for e in ENGS}
        self.count = {}
        self.sems = {}
        self.waited = {e: {} for e in ENGS}
        self.lastw = {}
        self.readers = {}
        self.same_engine_sync = True
        self.tag = ""
        self.alias = {}
        for e in ENGS:
            self._sem("E_" + e)

    def _sem(self, key):
        if key not in self.sems:
            self.sems[key] = self.stack.enter_context(self.nc.semaphore(key))
            self.count[key] = 0
        return self.sems[key]

    def _deps(self, eng, reads, writes):
        deps = {}

        def add(d):
            if d is not None and deps.get(d[0], 0) < d[1]:
                deps[d[0]] = d[1]

        for k in reads:
            add(self.lastw.get(k))
        for k in writes:
            add(self.lastw.get(k))
            for rk, rv in self.readers.get(k, {}).items():
                add((rk, rv))
        for k, v in deps.items():
            if k == "E_" + eng and (eng == "pe" or not self.same_engine_sync):
                continue
            if self.waited[eng].get(k, 0) >= v:
                continue
            self.waited[eng][k] = v
            self.streams[eng].append(("wait", k, v))

    def _mark(self, semkey, val, reads, writes):
        for k in writes:
            self.lastw[k] = (semkey, val)
            self.readers[k] = {}
        for k in reads:
            if k in writes:
                continue
            self.readers.setdefault(k, {})[semkey] = val

    def op(self, eng, fn, reads=(), writes=()):
        reads = tuple(self.alias.get(k, k) for k in reads)
        writes = tuple(self.alias.get(k, k) for k in writes)
        self._deps(eng, reads, writes)
        semkey = "E_" + eng
        self.count[semkey] += 1
        self.streams[eng].append(("op", fn, semkey, 1, self.tag))
        self._mark(semkey, self.count[semkey], reads, writes)

    def dma(self, eng, fn, semkey, reads=(), writes=(), inc=16):
        reads = tuple(self.alias.get(k, k) for k in reads)
        writes = tuple(self.alias.get(k, k) for k in writes)
        self._sem(semkey)
        self._deps(eng, reads, writes)
        self.count[semkey] += inc
        self.streams[eng].append(("op", fn, semkey, inc, self.tag))
        self._mark(semkey, self.count[semkey], reads, writes)

    def fence(self, keys, semkey):
        for k in keys:
            self.lastw[self.alias.get(k, k)] = (semkey, self.count[semkey])

    def wait_all(self, eng, keys):
        self._deps(eng, tuple(self.alias.get(k, k) for k in keys), ())

    def barrier(self):
        for e in ENGS:
            for k, v in self.count.items():
                if v > 0 and k != "E_" + e and self.waited[e].get(k, 0) < v:
                    self.waited[e][k] = v
                    self.streams[e].append(("wait", k, v))

    def replay(self, block):
        sched = self

        def run(engname):
            def body(e):
                for item in sched.streams[engname]:
                    if item[0] == "wait":
                        e.wait_ge(sched.sems[item[1]], item[2])
                    else:
                        _, fn, semkey, inc, _tag = item
                        fn(e).then_inc(sched.sems[semkey], inc)
            return body

        for name, sec in (("sp", block.sync), ("pe", block.tensor), ("act", block.scalar),
                          ("dve", block.vector), ("pool", block.gpsimd)):
            if self.streams[name]:
                sec(run(name))


def build_program(stage=STAGE):
    nc = bass.Bass("TRN2", target_bir_lowering=False)

    def din(name, shape):
        return nc.dram_tensor(name, list(shape), F32, kind="ExternalInput").ap()

    xT = din("xT", [P, KC, T])
    memT = din("memT", [P, KC, MEM])
    flag = din("flag", [P, 1])
    cols = din("cols", [P, NCOLS])
    sgrow = din("sgrow", [3, 512])
    sgw = din("sgw", [P, 4, P])
    w1g = din("w1g", [NFB, P, KC, 256])
    w1u = din("w1u", [NFB, P, KC, 256])
    w1d = din("w1d", [KC, P, FC, P])
    w2g = din("w2g", [NFB, P, KC, 256])
    w2u = din("w2u", [NFB, P, KC, 256])
    w2d = din("w2d", [KC, P, FC, P])
    win = din("win", [6, P, KC, 512])
    wout = din("wout", [2, P, KC, 512])
    wq = din("wq", [2, P, KC, 512])
    wk = din("wk", [2, P, KC, 512])
    wv = din("wv", [2, P, KC, 512])
    wo = din("wo", [2, P, KC, 512])
    outT = nc.dram_tensor("outT", [P, KC, T], F32, kind="ExternalOutput").ap()
    hsp = nc.dram_tensor("hsp", [NG, P, KC * GS], F32)
    hsb = nc.dram_tensor("hsb", [NG, P, KC * GS], BF16)
    osp = nc.dram_tensor("osp", [NG, P, 4 * GS], F32)
    qsp = nc.dram_tensor("qsp", [NG, P, 4 * GS], BF16)
    wcache = {nm: nc.dram_tensor("c_" + nm, shp, BF16) for nm, shp in (
        ("w1g", [NFB, P, KC * 256]), ("w1u", [NFB, P, KC * 256]), ("w1d", [KC, P, FC * P]),
        ("w2g", [NFB, P, KC * 256]), ("w2u", [NFB, P, KC * 256]), ("w2d", [KC, P, FC * P]))}
    cin = nc.dram_tensor("cin", [4 * P, P], F32)
    cout = nc.dram_tensor("cout", [8 * P, P], F32)

    with ExitStack() as st:
        S = Sched(nc, st)
        budget = [0]

        def sb(name, shape, dt):
            n = 1
            for s_ in shape[1:]:
                n *= s_
            budget[0] += n * (4 if dt == F32 else 2)
            return st.enter_context(nc.sbuf_tensor(name, list(shape), dt))

        H = sb("H", [P, KC, GS], F32)
        Hb2 = [sb("Hb_%d" % i, [P, KC, GS], BF16) for i in range(2)]
        OL = sb("OL", [P, 4, GS], F32)
        QP = sb("QP", [P, 4, GS], BF16)
        SCRB = sb("SCRB", [P, FC * GS], BF16)
        HGB = sb("HGB", [P, 19 * GS], BF16)
        TFR = sb("TFR", [P, 17 * GS], F32)
        mix = sb("mix", [P, KC, GS], BF16)
        WG = [sb("WG%d" % i, [P, KC, 256], BF16) for i in range(2)]
        WU = [sb("WU%d" % i, [P, KC, 256], BF16) for i in range(2)]
        WD = [sb("WD%d" % i, [P, FC, P], BF16) for i in range(2)]
        WB = [sb("WB%d" % i, [P, KC, 512], BF16) for i in range(2)]
        KT = sb("KT", [P, KC, MEM], BF16)
        Vm = sb("Vm", [P, 2, D], BF16)
        ident_f = sb("ident_f", [P, P], F32)
        ident_b = sb("ident_b", [P, P], BF16)
        onesK = sb("onesK", [P, P], F32)
        ones128 = sb("ones128", [P, P], F32)
        ones_b = sb("ones_b", [P, P], BF16)
        onesKb = sb("onesKb", [P, P], BF16)
        sqb = [sb("sqb%d" % i, [P, GS], BF16) for i in range(2)]
        MASK4 = sb("MASK4", [P, 4, P], BF16)
        rmask = sb("rmask", [P, GS], F32)
        WsT = sb("WsT", [P, 4, P], BF16)
        SGG = sb("SGG", [P, 512], F32)
        SGB = sb("SGB", [P, 512], F32)
        bs_hi = sb("bs_hi", [1, 512], BF16)
        bs_lo = sb("bs_lo", [1, 512], BF16)
        bs_lo2 = sb("bs_lo2", [1, 512], BF16)
        colt = sb("colt", [P, NCOLS], F32)
        colA = sb("colA", [P, 64], F32)
        lbc = sb("lbc", [P, 4], F32)
        omlc = sb("omlc", [P, 4], F32)
        flg = sb("flg", [P, 1], F32)
        Sst = sb("Sst", [P, 4, P], F32)
        Sbf = [[sb("Sbf%d_%d" % (h, i), [P, P], BF16) for i in range(2)] for h in range(4)]
        Sp = sb("Sp", [P, 4, P], F32)
        Spb = sb("Spb", [P, 4, P], BF16)
        LPX = sb("LPX", [P, 4, 9], F32)
        PXt = sb("PXt", [P, 4, 8], F32)
        EPS_AP = sb("epsc", [P, 1], F32)

        def tf(i, n=1):
            return TFR[:, i * GS:(i + n) * GS]
        tA, tB, tC, tD, tE, tF = (tf(i) for i in range(6))
        sl = [tf(6), tf(7)]
        E2 = [tf(8 + h) for h in range(4)]
        GU = TFR[:, 8 * GS:12 * GS].rearrange("p (g t) -> p g t", t=GS)
        lnT = [tf(12 + i) for i in range(5)]
        memf = TFR[:, 8 * GS:12 * GS].rearrange("p (c m) -> p c m", m=MEM)
        sgwt = TFR[:, 0:512].rearrange("p (g s) -> p g s", s=P)
        bsr = TFR[0:1, 2 * GS:3 * GS]
        bs_hf = TFR[0:1, 3 * GS:4 * GS]
        hmid = SCRB[:, :].rearrange("p (j t) -> p j t", t=GS)

        def bv(i, n=1):
            return SCRB[:, i * GS:(i + n) * GS]

        def hv(i, n=1):
            return HGB[:, i * GS:(i + n) * GS]
        qb = [hv(h) for h in range(4)]
        kdT = [hv(4 + h) for h in range(4)]
        scTs = [hv(8 + h).rearrange("p (i t) -> p i t", t=P) for h in range(4)]
        vt = hv(12, 4).rearrange("p (i f) -> p i f", f=512)
        qc, kc, kdec = hv(16), hv(17), hv(18)
        QT = bv(1, 8).rearrange("p (c t) -> p c t", t=GS)
        PT = [bv(9), bv(10)]
        memb = hv(0, 4).rearrange("p (c m) -> p c m", m=MEM)

        AL = S.alias
        for i_, n_ in enumerate(("tA", "tB", "tC", "tD", "tE", "tF", "sl0", "sl1")):
            AL[n_] = "tf%d" % i_
        for h_ in range(4):
            AL["E2_%d" % h_] = "tf%d" % (8 + h_)
            AL["GU%d" % h_] = "tf%d" % (8 + h_)
        for c_ in range(KC):
            AL["QT%d" % c_] = "scr%d" % (1 + c_)
        AL["PT0"], AL["PT1"] = "scr9", "scr10"
        AL["sgwt"], AL["bsr"], AL["bs_hf"] = "tf0", "tf2", "tf3"
        for j_ in range(FC):
            AL["hm%d" % j_] = "scr%d" % j_

        PS = [st.enter_context(nc.psum_tensor("PS%d" % i, [P, 512], F32)) for i in range(8)]
        pools = {"all": [0, 1, 2, 3, 4, 5], "A": [0, 1, 2, 3], "B": [4, 5]}
        rr = {"all": 0, "A": 0, "B": 0}
        ctx_ = {"pool": "all"}

        def bank():
            p_ = ctx_["pool"]
            i = pools[p_][rr[p_] % len(pools[p_])]
            rr[p_] += 1
            return PS[i], "PS%d" % i

        def CTX(tag, pool):
            S.tag = tag
            ctx_["pool"] = pool

        def MM(out, lhsT, rhs, start, stop, reads, writes):
            S.op("pe", lambda e: e.matmul(out, lhsT=lhsT, rhs=rhs, start=start, stop=stop),
                 reads=reads, writes=writes)

        def ACT(out, in_, func, reads, writes, scale=None, bias=None):
            kw = {}
            if scale is not None:
                kw["scale"] = scale
            if bias is not None:
                kw["bias"] = bias
            S.op("act", lambda e: e.activation(out=out, in_=in_, func=func, **kw), reads=reads, writes=writes)

        def TT(out, in0, in1, op, reads, writes, eng="dve"):
            S.op(eng, lambda e: e.tensor_tensor(out=out, in0=in0, in1=in1, op=op), reads=reads, writes=writes)

        def TS(out, in0, s1, s2, op0, op1, reads, writes, eng="dve"):
            if op1 is None:
                S.op(eng, lambda e: e.tensor_scalar(out=out, in0=in0, scalar1=s1, scalar2=None, op0=op0),
                     reads=reads, writes=writes)
            else:
                S.op(eng, lambda e: e.tensor_scalar(out=out, in0=in0, scalar1=s1, scalar2=s2, op0=op0, op1=op1),
                     reads=reads, writes=writes)

        def STT(out, in0, scalar, in1, op0, op1, reads, writes):
            S.op("dve", lambda e: e.scalar_tensor_tensor(out=out, in0=in0, scalar=scalar, in1=in1, op0=op0, op1=op1),
                 reads=reads, writes=writes)

        def DMA(eng, out, in_, sem, reads, writes):
            S.dma(eng, lambda e: e.dma_start(out=out, in_=in_), sem, reads=reads, writes=writes)

        def run(gen):
            for _ in gen:
                pass

        def interleave(ga, na, gb, nb):
            ia = ib = 0
            da = db = False
            while not (da and db):
                if not da and (db or ia * nb <= ib * na):
                    try:
                        next(ga)
                        ia += 1
                    except StopIteration:
                        da = True
                else:
                    try:
                        next(gb)
                        ib += 1
                    except StopIteration:
                        db = True

        Hk = ["H%d" % c for c in range(KC)]
        Hbk2 = [["Hb%d_%d" % (i, c) for c in range(KC)] for i in range(2)]
        mixk = ["mix%d" % c for c in range(KC)]

        setup_keys = ["colt", "SGG", "SGB", "bsr", "flg", "sgwt", "E2_0", "E2_1", "E2_2", "E2_3"]
        DMA("sp", colt[:], cols, "SETUP", [], ["colt"])
        DMA("sp", SGG[:], sgrow[0, :].partition_broadcast(P), "SETUP", [], ["SGG"])
        DMA("sp", SGB[:], sgrow[1, :].partition_broadcast(P), "SETUP", [], ["SGB"])
        DMA("sp", bsr, sgrow[2:3, :], "SETUP", [], ["bsr"])
        DMA("sp", flg[:], flag, "SETUP", [], ["flg"])
        DMA("sp", sgwt, sgw, "SETUP", [], ["sgwt"])
        DMA("sp", memf, memT, "SETUP", [], ["E2_0", "E2_1", "E2_2", "E2_3"])
        S.fence(setup_keys, "SETUP")

        S.op("pool", lambda e: e.memset(EPS_AP[:], EPS), writes=["epsc"])
        S.op("pool", lambda e: e.memset(ident_f[:], 0.0), writes=["ident_f"])
        S.op("pool", lambda e: e.affine_select(out=ident_f[:], in_=ident_f[:], pattern=[[-1, P]],
                                               compare_op=ALU.not_equal, fill=1.0, base=0, channel_multiplier=1),
             reads=["ident_f"], writes=["ident_f"])
        S.op("dve", lambda e: e.tensor_copy(out=ident_b[:], in_=ident_f[:]), reads=["ident_f"], writes=["ident_b"])
        S.op("pool", lambda e: e.memset(onesK[:], 1.0 / D), writes=["onesK"])
        S.op("pool", lambda e: e.memset(ones128[:], 1.0 / P), writes=["ones128"])
        S.op("pool", lambda e: e.memset(ones_b[:], 1.0), writes=["ones_b"])
        S.op("pool", lambda e: e.memset(onesKb[:], 1.0 / D), writes=["onesKb"])
        S.op("pool", lambda e: e.memset(rmask[:], 1.0), writes=["rmask"])
        S.op("pool", lambda e: e.memset(rmask[:].rearrange("p (c t) -> p c t", t=64)[:, :, 0:1], 0.0),
             reads=["rmask"], writes=["rmask"])
        S.op("pool", lambda e: e.memset(tB[:, 0:P], 1.0), writes=["tB"])
        S.op("pool", lambda e: e.affine_select(out=tB[:, 0:P], in_=tB[:, 0:P], pattern=[[1, P]],
                                               compare_op=ALU.is_ge, fill=0.0, base=0, channel_multiplier=-1),
             reads=["tB"], writes=["tB"])
        S.op("pool", lambda e: e.memset(tB[0:64, 64:P], 0.0), reads=["tB"], writes=["tB"])
        for i in range(4):
            S.op("dve", (lambda i: lambda e: e.tensor_copy(out=MASK4[:, i, :], in_=tB[:, 0:P]))(i),
                 reads=["tB"], writes=["MASK4"])
        S.op("pool", lambda e: e.memset(Sst[:], 0.0), writes=["S0", "S1", "S2", "S3"])
        for h in range(4):
            S.op("pool", (lambda h: lambda e: e.memset(Sbf[h][0][:], 0.0))(h), writes=["Sbf%d_0" % h])
        S.op("pool", lambda e: e.memset(LPX[:], 0.0), writes=["LPX0", "LPX1", "LPX2", "LPX3"])
        TS(colA[:], colt[:, 0:64], ALPHA, None, ALU.mult, None, ["colt"], ["colA"])
        TT(lbc[:], colt[:, C_LB0:C_LB0 + 4], colt[:, C_LB1:C_LB1 + 4], ALU.subtract, ["colt"], ["lbc"])
        ACT(lbc[:], lbc[:], AF.Sigmoid, ["lbc"], ["lbc"])
        TS(omlc[:], lbc[:], -1.0, 1.0, ALU.mult, ALU.add, ["lbc"], ["omlc"])
        S.op("dve", lambda e: e.tensor_copy(out=bs_hi[:], in_=bsr), reads=["bsr"], writes=["bs_hi"])
        S.op("dve", lambda e: e.tensor_copy(out=bs_hf, in_=bs_hi[:]), reads=["bs_hi"], writes=["bs_hf"])
        TT(bsr, bsr, bs_hf, ALU.subtract, ["bsr", "bs_hf"], ["bsr"])
        S.op("dve", lambda e: e.tensor_copy(out=bs_lo[:], in_=bsr), reads=["bsr"], writes=["bs_lo"])
        S.op("dve", lambda e: e.tensor_copy(out=bs_hf, in_=bs_lo[:]), reads=["bs_lo"], writes=["bs_hf"])
        TT(bs_lo2[:], bsr, bs_hf, ALU.subtract, ["bsr", "bs_hf"], ["bs_lo2"])
        for g_ in range(4):
            S.op("pool", (lambda g_: lambda e: e.affine_select(out=sgwt[:, g_, :], in_=sgwt[:, g_, :], pattern=[[-1, P]],
                                                               compare_op=ALU.is_ge, fill=0.0, base=0, channel_multiplier=1))(g_),
                 reads=["sgwt"], writes=["sgwt"])
        pb, pk = bank()
        for g_ in range(4):
            S.op("pe", (lambda g_: lambda e: e.transpose(pb[:, g_ * P:(g_ + 1) * P], in_=sgwt[:, g_, :], identity=ident_f[:]))(g_),
                 reads=["sgwt", "ident_f"], writes=[pk])
        S.op("dve", lambda e: e.tensor_copy(out=WsT[:], in_=pb[:, :].rearrange("p (g t) -> p g t", t=P)),
             reads=[pk], writes=["WsT"])

        def layer_norm(src, srck, N, gcol, bcol, tag, pool, out_f=None, out_fk=None, agcol=None, abcol=None,
                       out_b=None, out_bk=None):
            CTX(tag, pool)
            p1, k1 = bank()
            for c in range(KC):
                MM(p1[:, 0:N], onesK[:], src[:, c, :], c == 0, c == KC - 1, [srck[c], "onesK"], [k1])
            p2, k2 = bank()
            for c in range(KC):
                sq, sqk = sqb[c % 2][:, 0:N], "sqb%d" % (c % 2)
                ACT(sq, src[:, c, :], AF.Square, [srck[c]], [sqk])
                MM(p2[:, 0:N], onesKb[:], sq, c == 0, c == KC - 1, [sqk, "onesKb"], [k2])
            yield
            CTX(tag, pool)
            mean, rs_, nm_ = lnT[0][:, 0:N], lnT[1][:, 0:N], lnT[2][:, 0:N]
            ACT(mean, p1[:, 0:N], AF.Copy, [k1], ["lnA"])
            TT(rs_, mean, mean, ALU.mult, ["lnA"], ["lnB"])
            TT(rs_, p2[:, 0:N], rs_, ALU.subtract, [k2, "lnB"], ["lnB"])
            ACT(rs_, rs_, AF.Ln, ["lnB", "epsc"], ["lnB"], bias=EPS_AP[:, 0:1])
            ACT(rs_, rs_, AF.Exp, ["lnB"], ["lnB"], scale=-0.5)
            STT(nm_, mean, -1.0, rs_, ALU.mult, ALU.mult, ["lnA", "lnB"], ["lnC"])
            for c in range(KC):
                tmp, tk = (lnT[3], "lnD") if c % 2 == 0 else (lnT[4], "lnE")
                tmp = tmp[:, 0:N]
                TT(tmp, src[:, c, :], rs_, ALU.mult, [srck[c], "lnB"], [tk])
                TT(tmp, tmp, nm_, ALU.add, [tk, "lnC"], [tk])
                if out_b is not None:
                    ACT(out_b[:, c, :], tmp, AF.Identity, [tk, "colt"], [out_bk[c]],
                        scale=colt[:, gcol + c:gcol + c + 1], bias=colt[:, bcol + c:bcol + c + 1])
                if out_f is not None:
                    if agcol is not None:
                        sc_, bi_ = colA[:, agcol + c:agcol + c + 1], colA[:, abcol + c:abcol + c + 1]
                    else:
                        sc_, bi_ = colt[:, gcol + c:gcol + c + 1], colt[:, bcol + c:bcol + c + 1]
                    ACT(out_f[:, c, :], tmp, AF.Identity, [tk, "colt", "colA"], [out_fk[c]], scale=sc_, bias=bi_)
                if c % 2 == 1:
                    yield
                    CTX(tag, pool)

        memk = ["E2_%d" % (c // 2) for c in range(KC)]
        membk = ["qb%d" % (c // 2) for c in range(KC)]
        wload_ctr = [0]

        def load_wb(src_ap):
            i = wload_ctr[0] % 2
            wload_ctr[0] += 1
            DMA("pool", WB[i][:], src_ap, "WB%d" % i, [], ["WB%d" % i])
            return WB[i], "WB%d" % i

        def gen_setup_kv(pool):
            yield from layer_norm(memf, memk, MEM, C_LN["m"][0], C_LN["m"][1], "setup", pool, out_b=memb, out_bk=membk)
            for blk in range(2):
                CTX("setup", pool)
                wt, wkk = load_wb(wk[blk])
                for cc in range(4):
                    CTX("setup", pool)
                    fc_ = blk * 4 + cc
                    pb, pk = bank()
                    for k in range(KC):
                        MM(pb[:, 0:MEM], wt[:, k, cc * P:(cc + 1) * P], memb[:, k, :], k == 0, k == KC - 1, [wkk, membk[k]], [pk])
                    ACT(KT[:, fc_, :], pb[:, 0:MEM], AF.Copy, [pk], ["KT"])
                    yield
            for blk in range(2):
                CTX("setup", pool)
                wt, wkk = load_wb(wv[blk])
                for mt in range(2):
                    CTX("setup", pool)
                    pb, pk = bank()
                    for k in range(KC):
                        MM(pb[:, :], memb[:, k, mt * P:(mt + 1) * P], wt[:, k, :], k == 0, k == KC - 1, [wkk, membk[k]], [pk])
                    ACT(Vm[:, mt, blk * 512:(blk + 1) * 512], pb[:, :], AF.Copy, [pk], ["Vm"])
                    yield

        gu_ctr = [0]
        wd_ctr = [0]

        def wload(buf, bufk, src_d, nm, idx, first, pat, n_):
            ck = "wc_%s_%d" % (nm, idx)
            cview = wcache[nm][idx].rearrange(pat, **{pat.split("(")[1].split(")")[0].split()[1]: n_})
            if not USE_WCACHE:
                DMA("pool", buf[:], src_d[idx], bufk, [], [bufk])
            elif first:
                DMA("pool", buf[:], src_d[idx], bufk, [], [bufk])
                DMA("sp", cview, buf[:], "CW_" + bufk, [bufk], [ck])
            else:
                DMA("sp", buf[:], cview, bufk, [ck], [bufk])

        def ffn(wg_d, wu_d, wd_d, Hb, Hbk, pool, nm=None, first=True):
            yield from ffn_gu(wg_d, wu_d, Hb, Hbk, pool, nm, first)
            yield from ffn_down(wd_d, pool, nm, first)

        def ffn_gu(wg_d, wu_d, Hb, Hbk, pool, nm=None, first=True):
            for jb in range(NFB):
                CTX("ffn_gu", pool)
                b_ = gu_ctr[0] % 2
                gu_ctr[0] += 1
                wload(WG[b_], "WG%d" % b_, wg_d, nm + "g", jb, first, "p (c n) -> p c n", 256)
                wload(WU[b_], "WU%d" % b_, wu_d, nm + "u", jb, first, "p (c n) -> p c n", 256)
                for jj in range(2):
                    j = jb * 2 + jj
                    pg, pgk = bank()
                    for k in range(KC):
                        MM(pg[:, :], WG[b_][:, k, jj * P:(jj + 1) * P], Hb[:, k, :], k == 0, k == KC - 1,
                           ["WG%d" % b_, Hbk[k]], [pgk])
                    pu, puk = bank()
                    for k in range(KC):
                        MM(pu[:, :], WU[b_][:, k, jj * P:(jj + 1) * P], Hb[:, k, :], k == 0, k == KC - 1,
                           ["WU%d" % b_, Hbk[k]], [puk])
                    s_, sk_ = sl[j % 2], "sl%d" % (j % 2)
                    ACT(s_, pg[:, :], AF.Silu, [pgk], [sk_])
                    TT(hmid[:, j, :], s_, pu[:, :], ALU.mult, [sk_, puk], ["hm%d" % j])
                    yield
                    CTX("ffn_gu", pool)

        def ffn_down(wd_d, pool, nm=None, first=True):
            for c in range(KC):
                CTX("ffn_down", pool)
                b_ = wd_ctr[0] % 2
                wd_ctr[0] += 1
                wload(WD[b_], "WD%d" % b_, wd_d, nm + "d", c, first, "p (j n) -> p j n", P)
                po, pok = bank()
                for j in range(FC):
                    MM(po[:, :], WD[b_][:, j, :], hmid[:, j, :], j == 0, j == FC - 1, ["WD%d" % b_, "hm%d" % j], [pok])
                STT(H[:, c, :], po[:, :], 0.5, H[:, c, :], ALU.mult, ALU.add, [pok, Hk[c]], [Hk[c]])
                yield

        ln1_done = [False] * NG

        def gen_gu1(g, pool):
            gsl = slice(g * GS, (g + 1) * GS)
            Hb, Hbk = Hb2[g % 2], Hbk2[g % 2]
            CTX("xload", pool)
            DMA("pool", Hb[:], xT[:, :, gsl], "LXB_%d" % g, [], Hbk)
            yield
            yield from ffn_gu(w1g, w1u, Hb, Hbk, pool, "w1", g == 0)

        def gen_down1(g, pool):
            gsl = slice(g * GS, (g + 1) * GS)
            assert g == 0 or ln1_done[g - 1], "H still owned by the previous group's LayerNorm"
            CTX("xload", pool)
            DMA("sp", H[:], xT[:, :, gsl], "LHx_%d" % g, [], Hk)
            ACT(H[:], H[:], AF.Identity, Hk, Hk, scale=ALPHA)
            yield
            yield from ffn_down(w1d, pool, "w1", g == 0)

        def gen_ln1(g, pool):
            gsl = slice(g * GS, (g + 1) * GS)
            Hb, Hbk = Hb2[g % 2], Hbk2[g % 2]
            if stage == 1:
                yield from layer_norm(H, Hk, GS, C_LN[1][0], C_LN[1][1], "ln", pool, out_f=H, out_fk=Hk)
                DMA("sp", outT[:, :, gsl], H[:], "OUT_%d" % g, Hk, ["out%d" % g])
                ln1_done[g] = True
                return
            yield from layer_norm(H, Hk, GS, C_LN[1][0], C_LN[1][1], "ln", pool, out_f=H, out_fk=Hk,
                                  agcol=C_LN[1][0], abcol=C_LN[1][1], out_b=Hb, out_bk=Hbk)
            CTX("spill", pool)
            DMA("sp", hsp[g].rearrange("p (c t) -> p c t", t=GS), H[:], "SPL_%d" % g, Hk, ["hsp%d" % g])
            DMA("sp", hsb[g].rearrange("p (c t) -> p c t", t=GS), Hb[:], "SPL2_%d" % g, Hbk, ["hsb%d" % g])
            ln1_done[g] = True
            yield

        def chain(*gens):
            for g_ in gens:
                yield from g_

        cur = [0, 0, 0, 0]

        def gen_hg(g, pool):
            Hb, Hbk = Hb2[g % 2], Hbk2[g % 2]
            CTX("hg_prep", pool)
            wi_, wik = load_wb(win[2])
            for i in range(4):
                CTX("hg_prep", pool)
                pb, pk = bank()
                for k in range(KC):
                    MM(pb[:, :], Hb[:, k, i * P:(i + 1) * P], wi_[:, k, :], k == 0, k == KC - 1, [Hbk[k], wik], [pk])
                ACT(vt[:, i, :], pb[:, :], AF.Copy, [pk], ["vt%d" % i])
                yield
            CTX("hg_prep", pool)
            wq_, wqk = load_wb(win[0])
            wf_, wfk = load_wb(win[1])
            for h in range(4):
                CTX("hg_prep", pool)
                hs = slice(h * P, (h + 1) * P)
                pq, pqk = bank()
                for k in range(KC):
                    MM(pq[:, :], wq_[:, k, hs], Hb[:, k, :], k == 0, k == KC - 1, [wqk, Hbk[k]], [pqk])
                pf, pfk = bank()
                for k in range(KC):
                    MM(pf[:, :], wf_[:, k, hs], Hb[:, k, :], k == 0, k == KC - 1, [wfk, Hbk[k]], [pfk])
                ACT(tA, pf[:, :], AF.Sigmoid, [pfk], ["tA"])
                TS(tA, tA, omlc[:, h:h + 1], lbc[:, h:h + 1], ALU.mult, ALU.add, ["tA", "omlc", "lbc"], ["tA"])
                yield
                CTX("hg_prep", pool)
                ACT(tB, tA, AF.Ln, ["tA"], ["tB"])
                TS(tA, tA, -1.0, 1.0, ALU.mult, ALU.add, ["tA"], ["tA"])
                S.op("dve", lambda e: e.tensor_tensor_scan(out=tC, data0=rmask[:], data1=tB, initial=0.0,
                                                           op0=ALU.mult, op1=ALU.add),
                     reads=["rmask", "tB"], writes=["tC"])
                yield
                CTX("hg_prep", pool)
                b3 = tC.rearrange("p (c t) -> p c t", t=64)
                TT(tD.rearrange("p (c t) -> p c t", t=64), b3, b3[:, :, 31:32].to_broadcast([P, 8, 64]), ALU.subtract,
                   ["tC"], ["tD"])
                TT(tE.rearrange("p (c t) -> p c t", t=64), b3[:, :, 63:64].to_broadcast([P, 8, 64]), b3, ALU.subtract,
                   ["tC"], ["tE"])
                ACT(tB, tD, AF.Exp, ["tD"], ["tB"])
                TT(qc, pq[:, :], tB, ALU.mult, [pqk, "tB"], ["qc"])
                yield
                CTX("hg_prep", pool)
                ACT(tF, tD, AF.Exp, ["tD"], ["tF"], scale=-1.0)
                TT(kc, tA, tF, ALU.mult, ["tA", "tF"], ["kc"])
                yield
                CTX("hg_prep", pool)
                ACT(E2[h], tC, AF.Exp, ["tC"], ["E2_%d" % h])
                TT(qb[h], pq[:, :], E2[h], ALU.mult, [pqk, "E2_%d" % h], ["qb%d" % h])
                yield
                CTX("hg_prep", pool)
                ACT(tE, tE, AF.Exp, ["tE"], ["tE"])
                TT(kdec, tA, tE, ALU.mult, ["tA", "tE"], ["kdec"])
                yield
                CTX("hg_prep", pool)
                S.op("dve", (lambda h: lambda e: e.tensor_tensor_scan(
                    out=LPX[:, h, 1:9], data0=rmask[:, 1:9], data1=tC.rearrange("p (c t) -> p c t", t=64)[:, :, 63],
                    initial=LPX[:, h, 0:1], op0=ALU.mult, op1=ALU.add))(h),
                    reads=["rmask", "tC", "LPX%d" % h], writes=["LPX%d" % h])
                ACT(PXt[:, h, :], LPX[:, h, 0:8], AF.Exp, ["LPX%d" % h], ["PXt%d" % h])
                TT(QP[:, h, :].rearrange("p (c t) -> p c t", t=64), qb[h].rearrange("p (c t) -> p c t", t=64),
                   PXt[:, h, :].unsqueeze(2).to_broadcast([P, 8, 64]), ALU.mult,
                   ["qb%d" % h, "PXt%d" % h], ["QP%d" % h])
                S.op("dve", (lambda h: lambda e: e.tensor_copy(out=LPX[:, h, 0:1], in_=LPX[:, h, 8:9]))(h),
                     reads=["LPX%d" % h], writes=["LPX%d" % h])
                yield
                CTX("hg_prep", pool)
                pt, ptk = bank()
                ptb = pt[:, :].bitcast(BF16)
                for i in range(4):
                    S.op("pe", (lambda i, ptb: lambda e: e.transpose(ptb[:, i * P:(i + 1) * P], in_=kdec[:, i * P:(i + 1) * P],
                                                                      identity=ident_b[:]))(i, ptb),
                         reads=["kdec", "ident_b"], writes=[ptk])
                ACT(kdT[h], ptb[:, 0:GS], AF.Copy, [ptk], ["kdT%d" % h])
                yield
                CTX("hg_prep", pool)
                psc, psck = bank()
                for i in range(4):
                    ts_ = slice(i * P, (i + 1) * P)
                    MM(psc[:, ts_], kc[:, ts_], qc[:, ts_], True, True, ["kc", "qc"], [psck])
                TT(scTs[h], psc[:, :].rearrange("p (i t) -> p i t", t=P), MASK4[:], ALU.mult, [psck, "MASK4"], ["scTs%d" % h])
                yield
            for i in range(4):
                CTX("hg_scan", pool)
                po, pok = bank()
                for half in range(2):
                    c = 2 * i + half
                    gc = g * 8 + c
                    rows = slice(half * 64, half * 64 + 64)
                    cols_ = slice(i * P + half * 64, i * P + half * 64 + 64)
                    for h in range(4):
                        hs = slice(h * P, (h + 1) * P)
                        slot = PS[6 + gc % 2][:, hs]
                        MM(slot, kdT[h][rows, i * P:(i + 1) * P], vt[rows, i, hs], True, True,
                           ["kdT%d" % h, "vt%d" % i], ["PSd%d" % (gc % 2)])
                    for h in range(4):
                        hs = slice(h * P, (h + 1) * P)
                        ocol = slice(h * P + half * 64, h * P + half * 64 + 64)
                        MM(po[:, ocol], vt[rows, i, hs], scTs[h][rows, i, half * 64:half * 64 + 64], True, False,
                           ["vt%d" % i, "scTs%d" % h], [pok])
                        MM(po[:, ocol], Sbf[h][cur[h]][:], qb[h][:, cols_], False, True,
                           ["Sbf%d_%d" % (h, cur[h]), "qb%d" % h], [pok])
                    for h in range(4):
                        slot = PS[6 + gc % 2][:, h * P:(h + 1) * P]
                        STT(Sst[:, h, :], Sst[:, h, :], E2[h][:, cols_.stop - 1:cols_.stop], slot, ALU.mult, ALU.add,
                            ["S%d" % h, "E2_%d" % h, "PSd%d" % (gc % 2)], ["S%d" % h])
                        ACT(Sbf[h][1 - cur[h]][:], Sst[:, h, :], AF.Copy, ["S%d" % h], ["Sbf%d_%d" % (h, 1 - cur[h])])
                        cur[h] = 1 - cur[h]
                ACT(OL[:, :, i * P:(i + 1) * P], po[:, :].rearrange("p (h t) -> p h t", t=P), AF.Copy,
                    [pok], ["OL%d" % i])
                yield
            CTX("spill", pool)
            DMA("sp", osp[g].rearrange("p (h t) -> p h t", t=GS), OL[:], "SPL3_%d" % g, ["OL%d" % i for i in range(4)], ["osp%d" % g])
            DMA("sp", qsp[g].rearrange("p (h t) -> p h t", t=GS), QP[:], "SPL4_%d" % g, ["QP%d" % h for h in range(4)], ["qsp%d" % g])
            yield

        vln4_std = hv(0, 4).rearrange("p (i f) -> p i f", f=512)
        vln4_alt = bv(13, 4).rearrange("p (i f) -> p i f", f=512)
        bst16 = sb("bst16", [P, 16, 6], F32)
        bmv16 = sb("bmv16", [P, 16, 2], F32)
        r16 = sb("r16", [P, 16], F32)
        n16 = sb("n16", [P, 16], F32)
        gV = gV_std = [tA, tB, tC, tD]
        gVk = gVk_std = ["tA", "tB", "tC", "tD"]
        sqs = [(tE, "tE"), (tF, "tF"), (sl[0], "sl0"), (sl[1], "sl1")]

        GUalt = TFR[:, 12 * GS:16 * GS].rearrange("p (g t) -> p g t", t=GS)
        GUaltk = ["lnA", "lnB", "lnC", "lnD"]
        GUstd = (GU, ["GU%d" % gg for gg in range(4)])

        def gen_mixer(g, pool, gu=None, delay=0):
            yield from gen_mixer_sgu(g, pool, gu, delay)
            yield from gen_mixer_fin(g, pool, gu)

        def gen_mixer_sgu(g, pool, gu=None, delay=0, vl=None, altw=False, gv=None):
            for _ in range(delay):
                yield
            GU, GUk = gu if gu is not None else GUstd
            gV, gVk = gv if gv is not None else (gV_std, gVk_std)
            vln4, vlk = vl if vl is not None else (vln4_std, ["qb%d" % i for i in range(4)])
            Hb, Hbk = Hb2[g % 2], Hbk2[g % 2]
            CTX("mix_load", pool)
            DMA("sp", Hb[:], hsb[g].rearrange("p (c t) -> p c t", t=GS), "LHB_%d" % g, ["hsb%d" % g], Hbk)
            yield
            CTX("sgu", pool)
            if altw:
                for hh in range(2):
                    DMA("pool", WG[hh][:], win[4][:, :, hh * 256:(hh + 1) * 256], "WG%d" % hh, [], ["WG%d" % hh])
            else:
                wu_, wuk = load_wb(win[4])
            for gg in range(4):
                CTX("sgu", pool)
                pu, puk = bank()
                for k in range(KC):
                    if altw:
                        lw, lwk = WG[gg // 2][:, k, (gg % 2) * P:(gg % 2 + 1) * P], "WG%d" % (gg // 2)
                    else:
                        lw, lwk = wu_[:, k, gg * P:(gg + 1) * P], wuk
                    MM(pu[:, :], lw, Hb[:, k, :], k == 0, k == KC - 1, [lwk, Hbk[k]], [puk])
                ACT(GU[:, gg, :], pu[:, :], AF.Gelu_apprx_tanh, [puk], [GUk[gg]])
                yield
            CTX("sgu", pool)
            if altw:
                for hh in range(2):
                    DMA("pool", WU[hh][:], win[5][:, :, hh * 256:(hh + 1) * 256], "WU%d" % hh, [], ["WU%d" % hh])
            else:
                wv_, wvk = load_wb(win[5])
            for i in range(4):
                CTX("sgu", pool)
                ts_ = slice(i * P, (i + 1) * P)
                pv, pvk = bank()
                if altw:
                    for hh in range(2):
                        for k in range(KC):
                            MM(pv[:, hh * 256:(hh + 1) * 256], Hb[:, k, ts_], WU[hh][:, k, :], k == 0, k == KC - 1,
                               [Hbk[k], "WU%d" % hh], [pvk])
                else:
                    for k in range(KC):
                        MM(pv[:, :], Hb[:, k, ts_], wv_[:, k, :], k == 0, k == KC - 1, [Hbk[k], wvk], [pvk])
                ACT(gV[i], pv[:, :], AF.Gelu_apprx_tanh, [pvk], [gVk[i]])
                for gg in range(4):
                    S.op("dve", (lambda i, gg: lambda e: e.bn_stats(out=bst16[:, i * 4 + gg, :], in_=gV[i][:, gg * P:(gg + 1) * P]))(i, gg),
                         reads=[gVk[i]], writes=["bst16"])
                for gg in range(4):
                    S.op("dve", (lambda i, gg: lambda e: e.bn_aggr(out=bmv16[:, i * 4 + gg, :], in_=bst16[:, i * 4 + gg, :]))(i, gg),
                         reads=["bst16"], writes=["bmv16"])
                yield
            CTX("sgu", pool)
            ACT(r16[:], bmv16[:, :, 1], AF.Ln, ["bmv16", "epsc"], ["r16"], bias=EPS_AP[:, 0:1])
            ACT(r16[:], r16[:], AF.Exp, ["r16"], ["r16"], scale=-0.5)
            STT(n16[:], bmv16[:, :, 0], -1.0, r16[:], ALU.mult, ALU.mult, ["bmv16", "r16"], ["n16"])
            for i in range(4):
                CTX("sgu", pool)
                for gg in range(4):
                    j_ = i * 4 + gg
                    ACT(gV[i][:, gg * P:(gg + 1) * P], gV[i][:, gg * P:(gg + 1) * P], AF.Identity, [gVk[i], "r16", "n16"], [gVk[i]],
                        scale=r16[:, j_:j_ + 1], bias=n16[:, j_:j_ + 1])
                TT(gV[i], gV[i], SGG[:], ALU.mult, [gVk[i], "SGG"], [gVk[i]])
                TT(vln4[:, i, :], gV[i], SGB[:], ALU.add, [gVk[i], "SGB"], [vlk[i]])
                yield
            for i in range(4):
                CTX("sgu", pool)
                ts_ = slice(i * P, (i + 1) * P)
                ps_, psk = bank()
                for gg in range(4):
                    gs_ = slice(gg * P, (gg + 1) * P)
                    MM(ps_[:, gs_], vln4[:, i, gs_], WsT[:, gg, :], True, False, [vlk[i], "WsT"], [psk])
                    MM(ps_[:, gs_], ones_b[0:1, :], bs_hi[0:1, gs_], False, False, ["ones_b", "bs_hi"], [psk])
                    MM(ps_[:, gs_], ones_b[0:1, :], bs_lo[0:1, gs_], False, False, ["ones_b", "bs_lo"], [psk])
                    MM(ps_[:, gs_], ones_b[0:1, :], bs_lo2[0:1, gs_], False, True, ["ones_b", "bs_lo2"], [psk])
                TT(mix[:, 4:8, ts_], ps_[:, :].rearrange("p (g t) -> p g t", t=P), GU[:, :, ts_], ALU.mult,
                   [psk] + list(GUk), mixk[4:8])
                yield

        def gen_mixer_fin(g, pool, gu=None):
            GU, GUk = gu if gu is not None else GUstd
            Hb, Hbk = Hb2[g % 2], Hbk2[g % 2]
            CTX("hg_fin", pool)
            DMA("sp", OL[:], osp[g].rearrange("p (h t) -> p h t", t=GS), "LOL_%d" % g, ["osp%d" % g], ["OL%d" % i for i in range(4)])
            DMA("sp", QP[:], qsp[g].rearrange("p (h t) -> p h t", t=GS), "LQP_%d" % g, ["qsp%d" % g], ["QP%d" % h for h in range(4)])
            wg_, wgk = load_wb(win[3])
            for h in range(4):
                CTX("hg_fin", pool)
                pc, pck = bank()
                MM(pc[:, :], Spb[:, h, :], QP[:, h, :], True, True, ["Spb", "QP%d" % h], [pck])
                TT(gV[h], OL[:, h, :], pc[:, :], ALU.add, ["OL%d" % i for i in range(4)] + [pck], [gVk[h]])
                ACT(sqs[h][0], gV[h], AF.Square, [gVk[h]], [sqs[h][1]])
                yield
            for h in range(4):
                CTX("hg_fin", pool)
                pm, pmk = bank()
                MM(pm[:, :], ones128[:], sqs[h][0], True, True, ["ones128", sqs[h][1]], [pmk])
                ACT(sqs[h][0], pm[:, :], AF.Ln, [pmk, "epsc"], [sqs[h][1]], bias=EPS_AP[:, 0:1])
                yield
            for h in range(4):
                CTX("hg_fin", pool)
                ACT(sqs[h][0], sqs[h][0], AF.Exp, [sqs[h][1]], [sqs[h][1]], scale=-0.5)
                TT(gV[h], gV[h], sqs[h][0], ALU.mult, [gVk[h], sqs[h][1]], [gVk[h]])
            yield
            for h in range(4):
                CTX("hg_fin", pool)
                hs = slice(h * P, (h + 1) * P)
                pg, pgk = bank()
                for k in range(KC):
                    MM(pg[:, :], wg_[:, k, hs], Hb[:, k, :], k == 0, k == KC - 1, [wgk, Hbk[k]], [pgk])
                ACT(GU[:, h, :], pg[:, :], AF.Silu, [pgk], [GUk[h]])
                STT(mix[:, h, :], gV[h], colt[:, C_HGN:C_HGN + 1], GU[:, h, :], ALU.mult, ALU.mult,
                    [gVk[h], GUk[h], "colt"], [mixk[h]])
                yield

        def gen_rest(g, pool):
            gsl = slice(g * GS, (g + 1) * GS)
            Hb, Hbk = Hb2[g % 2], Hbk2[g % 2]
            CTX("w_out", pool)
            DMA("sp", H[:], hsp[g].rearrange("p (c t) -> p c t", t=GS), "LHs_%d" % g, ["hsp%d" % g], Hk)
            if stage == 20:
                ACT(H[:], mix[:], AF.Copy, mixk, Hk)
                DMA("sp", outT[:, :, gsl], H[:], "OUT_%d" % g, Hk, ["out%d" % g])
                return
            wo0, wo0k = load_wb(wout[0])
            wo1, wo1k = load_wb(wout[1])
            for c in range(KC):
                wt, wtk = (wo0, wo0k) if c < 4 else (wo1, wo1k)
                cs = slice((c % 4) * P, (c % 4 + 1) * P)
                po, pok = bank()
                for k in range(KC):
                    MM(po[:, :], wt[:, k, cs], mix[:, k, :], k == 0, k == KC - 1, [wtk, mixk[k]], [pok])
                TT(H[:, c, :], po[:, :], H[:, c, :], ALU.add, [pok, Hk[c]], [Hk[c]])
            yield
            if stage == 2:
                yield from layer_norm(H, Hk, GS, C_LN[2][0], C_LN[2][1], "ln", pool, out_f=H, out_fk=Hk)
                DMA("sp", outT[:, :, gsl], H[:], "OUT_%d" % g, Hk, ["out%d" % g])
                return
            yield from layer_norm(H, Hk, GS, C_LN[2][0], C_LN[2][1], "ln", pool, out_f=H, out_fk=Hk,
                                  agcol=C_LN[2][0], abcol=C_LN[2][1], out_b=Hb, out_bk=Hbk)
            CTX("xattn", pool)
            wq0, wq0k = load_wb(wq[0])
            wq1, wq1k = load_wb(wq[1])
            for fc_ in range(KC):
                wt, wtk = (wq0, wq0k) if fc_ < 4 else (wq1, wq1k)
                cs = slice((fc_ % 4) * P, (fc_ % 4 + 1) * P)
                pq, pqk = bank()
                for k in range(KC):
                    MM(pq[:, :], wt[:, k, cs], Hb[:, k, :], k == 0, k == KC - 1, [wtk, Hbk[k]], [pqk])
                ACT(QT[:, fc_, :], pq[:, :], AF.Copy, [pqk], ["QT%d" % fc_])
            yield
            CTX("xattn", pool)
            for h in range(4):
                for mt in range(2):
                    psc, psck = bank()
                    for ec in range(2):
                        MM(psc[:, :], KT[:, h * 2 + ec, mt * P:(mt + 1) * P], QT[:, h * 2 + ec, :], ec == 0, ec == 1,
                           ["KT", "QT%d" % (h * 2 + ec)], [psck])
                    ACT(PT[mt], psc[:, :], AF.Exp, [psck], ["PT%d" % mt], scale=1.0 / 16.0)
                psm, psmk = bank()
                for mt in range(2):
                    MM(psm[:, :], ones_b[:], PT[mt], mt == 0, mt == 1, ["ones_b", "PT%d" % mt], [psmk])
                ACT(tA, psm[:, :], AF.Ln, [psmk], ["tA"])
                ACT(tA, tA, AF.Exp, ["tA"], ["tA"], scale=-1.0)
                for ec in range(2):
                    pv, pvk = bank()
                    for mt in range(2):
                        MM(pv[:, :], Vm[:, mt, (h * 2 + ec) * P:(h * 2 + ec + 1) * P], PT[mt], mt == 0, mt == 1,
                           ["Vm", "PT%d" % mt], [pvk])
                    TT(mix[:, h * 2 + ec, :], pv[:, :], tA, ALU.mult, [pvk, "tA"], [mixk[h * 2 + ec]])
            yield
            CTX("xattn", pool)
            wo0, wo0k = load_wb(wo[0])
            wo1, wo1k = load_wb(wo[1])
            for c in range(KC):
                wt, wtk = (wo0, wo0k) if c < 4 else (wo1, wo1k)
                cs = slice((c % 4) * P, (c % 4 + 1) * P)
                po, pok = bank()
                for k in range(KC):
                    MM(po[:, :], wt[:, k, cs], mix[:, k, :], k == 0, k == KC - 1, [wtk, mixk[k]], [pok])
                TT(H[:, c, :], po[:, :], H[:, c, :], ALU.add, [pok, Hk[c]], [Hk[c]])
            yield
            if stage == 3:
                yield from layer_norm(H, Hk, GS, C_LN[3][0], C_LN[3][1], "ln", pool, out_f=H, out_fk=Hk)
                DMA("sp", outT[:, :, gsl], H[:], "OUT_%d" % g, Hk, ["out%d" % g])
                return
            yield from layer_norm(H, Hk, GS, C_LN[3][0], C_LN[3][1], "ln", pool, out_f=H, out_fk=Hk,
                                  agcol=C_LN[3][0], abcol=C_LN[3][1], out_b=Hb, out_bk=Hbk)

        def gen_ffn2(g, pool):
            gsl = slice(g * GS, (g + 1) * GS)
            Hb, Hbk = Hb2[g % 2], Hbk2[g % 2]
            yield from ffn(w2g, w2u, w2d, Hb, Hbk, pool, "w2", g == 0)
            yield from layer_norm(H, Hk, GS, C_LN[4][0], C_LN[4][1], "ln", pool, out_f=H, out_fk=Hk)
            CTX("store", pool)
            DMA("sp", outT[:, :, gsl], H[:], "OUT_%d" % g, Hk, ["out%d" % g])
            yield

        interleave(chain(gen_gu1(0, "A"), gen_down1(0, "A")), 32, gen_setup_kv("B"), 15)
        if stage >= 2:
            for g in range(NG):
                A = chain(gen_ln1(g, "B"), gen_hg(g, "B"))
                if g + 1 < NG:
                    interleave(A, 54, chain(gen_gu1(g + 1, "A"), gen_down1(g + 1, "A")), 32)
                else:
                    interleave(A, 52, gen_mixer_sgu(0, "A", (GUalt, GUaltk), delay=8, vl=(vln4_alt, ["scr%d" % (13 + i) for i in range(4)]),
                                                    altw=True, gv=([H[:, c_, :] for c_ in range(4)], Hk[0:4])), 26)
        else:
            run(gen_ln1(0, "all"))
            for g in range(1, NG):
                run(chain(gen_gu1(g, "all"), gen_down1(g, "all"), gen_ln1(g, "all")))

        if stage >= 2:
            CTX("xchg", "all")
            DMA("pool", cin.ap().rearrange("(h d) v -> d h v", d=P), Sst[:], "XC", ["S0", "S1", "S2", "S3"], ["cin"])
            S.dma("pool", lambda e: e.collective_compute("AllGather", ALU.bypass, replica_groups=[[0, 1], [2, 3], [4, 5], [6, 7]],
                                                         ins=[cin.ap().opt()], outs=[cout.ap().opt()]),
                  "CC", reads=["cin"], writes=["cout"], inc=1)
            DMA("pool", Sp[:], cout.ap()[0:4 * P, :].rearrange("(h d) v -> d h v", d=P), "XC", ["cout"], ["Sp"])
            TS(Spb[:], Sp[:], flg[:, 0:1], None, ALU.mult, None, ["Sp", "flg"], ["Spb"])

            if stage >= 4:
                run(gen_mixer_fin(0, "all", (GUalt, GUaltk)))
            else:
                run(gen_mixer(0, "all"))
            for g in range(NG):
                run(gen_rest(g, "all"))
                if stage in (2, 3, 20):
                    if g + 1 < NG:
                        run(gen_mixer(g + 1, "all"))
                    continue
                if g + 1 < NG:
                    interleave(gen_ffn2(g, "A"), 35, gen_mixer(g + 1, "B"), 31)
                else:
                    run(gen_ffn2(g, "all"))

        S.wait_all("sp", ["out%d" % g for g in range(NG)])
        S.wait_all("act", ["out%d" % g for g in range(NG)])
        build_program.last_sched = S
        print("SBUF bytes/partition:", budget[0], "instr:", {e: len(v) for e, v in S.streams.items()})
        with nc.Block() as block:
            S.replay(block)
    return nc


def _blk_kn(w, nb):
    K, N = w.shape
    return np.ascontiguousarray(w.reshape(K // P, P, N // nb, nb).transpose(2, 1, 0, 3))


def prepare_inputs(x, mem, ffn1_w_gate, ffn1_w_up, ffn1_w_down, ln1_g, ln1_b, w_in, hg_lb_logits, hg_norm_g,
                   sg_ln_g, sg_ln_b, sg_w_s, sg_b_s, w_out, ln2_g, ln2_b, mem_ln_g, mem_ln_b, xa_w_q, xa_w_k,
                   xa_w_v, xa_w_o, ln3_g, ln3_b, ffn2_w_gate, ffn2_w_up, ffn2_w_down, ln4_g, ln4_b):
    f = lambda a: np.asarray(a, dtype=np.float32)
    col = lambda v: f(v).reshape(KC, P).T
    cols = np.zeros((P, NCOLS), np.float32)
    for idx, (g_, b_) in ((1, (ln1_g, ln1_b)), (2, (ln2_g, ln2_b)), (3, (ln3_g, ln3_b)), (4, (ln4_g, ln4_b)),
                          ("m", (mem_ln_g, mem_ln_b))):
        cols[:, C_LN[idx][0]:C_LN[idx][0] + 8] = col(g_[0])
        cols[:, C_LN[idx][1]:C_LN[idx][1] + 8] = col(b_[0])
    lg = f(hg_lb_logits)
    cols[:, C_LB0:C_LB0 + 4] = lg[0].T
    cols[:, C_LB1:C_LB1 + 4] = lg[1].T
    cols[:, C_HGN] = f(hg_norm_g)[0]
    sgrow = np.stack([f(sg_ln_g)[0].reshape(512), f(sg_ln_b)[0].reshape(512), f(sg_b_s)[0].reshape(512)])
    sgw = np.ascontiguousarray(f(sg_w_s)[0].transpose(1, 0, 2))
    shared = {
        "cols": cols, "sgrow": np.ascontiguousarray(sgrow), "sgw": sgw,
        "w1g": _blk_kn(f(ffn1_w_gate)[0], 256), "w1u": _blk_kn(f(ffn1_w_up)[0], 256),
        "w1d": _blk_kn(f(ffn1_w_down)[0], P),
        "w2g": _blk_kn(f(ffn2_w_gate)[0], 256), "w2u": _blk_kn(f(ffn2_w_up)[0], 256),
        "w2d": _blk_kn(f(ffn2_w_down)[0], P),
        "win": _blk_kn(f(w_in)[0], 512), "wout": _blk_kn(f(w_out)[0], 512),
        "wq": _blk_kn(f(xa_w_q)[0], 512), "wk": _blk_kn(f(xa_w_k)[0], 512),
        "wv": _blk_kn(f(xa_w_v)[0], 512), "wo": _blk_kn(f(xa_w_o)[0], 512),
    }
    x = f(x)
    mem = f(mem)
    in_maps = []
    for c in range(NCORES):
        b, half = c // 2, c % 2
        xs = x[b, half * T:(half + 1) * T, :]
        m = dict(shared)
        m["xT"] = np.ascontiguousarray(xs.reshape(T, KC, P).transpose(2, 1, 0))
        m["memT"] = np.ascontiguousarray(mem[b].reshape(MEM, KC, P).transpose(2, 1, 0))
        m["flag"] = np.full((P, 1), float(half), np.float32)
        in_maps.append(m)
    return in_maps


def assemble(results):
    out = np.zeros((4, 2 * T, D), np.float32)
    for c in range(NCORES):
        b, half = c // 2, c % 2
        o = np.asarray(results[c]["outT"])
        out[b, half * T:(half + 1) * T, :] = o.transpose(2, 1, 0).reshape(T, D)
    return out


def kernel(**inputs):
    in_maps = prepare_inputs(**inputs)
    nc = build_program(STAGE)
    res = run_bass_kernel_spmd(nc, in_maps, core_ids=list(range(NCORES)))
    return assemble(res.results)
```

```python
import numpy as np
from contextlib import ExitStack
import concourse.bass as bass
import concourse.mybir as mybir
from concourse.bass_utils import run_bass_kernel_spmd

F32 = mybir.dt.float32
BF16 = mybir.dt.bfloat16
AF = mybir.ActivationFunctionType
ALU = mybir.AluOpType

NCORES = 8
P = 128
D = 1024
KC = 8
DFF = 2816
FC = 22
NFB = 11
T = 2048
GS = 512
NG = 4
MEM = 256
ALPHA = 2.0 ** 0.25
EPS = 1e-5
STAGE = 4
USE_WCACHE = False

C_LN = {1: (0, 8), 2: (16, 24), 3: (32, 40), 4: (48, 56), "m": (64, 72)}
C_LB0, C_LB1, C_HGN, NCOLS = 80, 84, 88, 89

ENGS = ("pe", "act", "dve", "pool", "sp")


class Sched:
    def __init__(self, nc, stack):
        self.nc = nc
        self.stack = stack
        self.streams = {e: [] for e in ENGS}
        self.count = {}
        self.sems = {}
        self.waited = {e: {} for e in ENGS}
        self.lastw = {}
        self.readers = {}
        self.same_engine_sync = True
        self.tag = ""
        self.alias = {}
        for e in ENGS:
            self._sem("E_" + e)

    def _sem(self, key):
        if key not in self.sems:
            self.sems[key] = self.stack.enter_context(self.nc.semaphore(key))
            self.count[key] = 0
        return self.sems[key]

    def _deps(self, eng, reads, writes):
        deps = {}

        def add(d):
            if d is not None and deps.get(d[0], 0) < d[1]:
                deps[d[0]] = d[1]

        for k in reads:
            add(self.lastw.get(k))
        for k in writes:
            add(self.lastw.get(k))
            for rk, rv in self.readers.get(k, {}).items():
                add((rk, rv))
        for k, v in deps.items():
            if k == "E_" + eng and (eng == "pe" or not self.same_engine_sync):
                continue
            if self.waited[eng].get(k, 0) >= v:
                continue
            self.waited[eng][k] = v
            self.streams[eng].append(("wait", k, v))

    def _mark(self, semkey, val, reads, writes):
        for k in writes:
            self.lastw[k] = (semkey, val)
            self.readers[k] = {}
        for k in reads:
            if k in writes:
                continue
            self.readers.setdefault(k, {})[semkey] = val

    def op(self, eng, fn, reads=(), writes=()):
        reads = tuple(self.alias.get(k, k) for k in reads)
        writes = tuple(self.alias.get(k, k) for k in writes)
        self._deps(eng, reads, writes)
        semkey = "E_" + eng
        self.count[semkey] += 1
        self.streams[eng].append(("op", fn, semkey, 1, self.tag))
        self._mark(semkey, self.count[semkey], reads, writes)

    def dma(self, eng, fn, semkey, reads=(), writes=(), inc=16):
        reads = tuple(self.alias.get(k, k) for k in reads)
        writes = tuple(self.alias.get(k, k) for k in writes)
        self._sem(semkey)
        self._deps(eng, reads, writes)
        self.count[semkey] += inc
        self.streams[eng].append(("op", fn, semkey, inc, self.tag))
        self._mark(semkey, self.count[semkey], reads, writes)

    def fence(self, keys, semkey):
        for k in keys:
            self.lastw[self.alias.get(k, k)] = (semkey, self.count[semkey])

    def wait_all(self, eng, keys):
        self._deps(eng, tuple(self.alias.get(k, k) for k in keys), ())

    def barrier(self):
        for e in ENGS:
            for k, v in self.count.items():
                if v > 0 and k != "E_" + e and self.waited[e].get(k, 0) < v:
                    self.waited[e][k] = v
                    self.streams[e].append(("wait", k, v))

    def replay(self, block):
        sched = self

        def run(engname):
            def body(e):
                for item in sched.streams[engname]:
                    if item[0] == "wait":
                        e.wait_ge(sched.sems[item[1]], item[2])
                    else:
                        _, fn, semkey, inc, _tag = item
                        fn(e).then_inc(sched.sems[semkey], inc)
            return body

        for name, sec in (("sp", block.sync), ("pe", block.tensor), ("act", block.scalar),
                          ("dve", block.vector), ("pool", block.gpsimd)):
            if self.streams[name]:
                sec(run(name))


def build_program(stage=STAGE):
    nc = bass.Bass("TRN2", target_bir_lowering=False)

    def din(name, shape):
        return nc.dram_tensor(name, list(shape), F32, kind="ExternalInput").ap()

    xT = din("xT", [P, KC, T])
    memT = din("memT", [P, KC, MEM])
    flag = din("flag", [P, 1])
    cols = din("cols", [P, NCOLS])
    sgrow = din("sgrow", [3, 512])
    sgw = din("sgw", [P, 4, P])
    w1g = din("w1g", [NFB, P, KC, 256])
    w1u = din("w1u", [NFB, P, KC, 256])
    w1d = din("w1d", [KC, P, FC, P])
    w2g = din("w2g", [NFB, P, KC, 256])
    w2u = din("w2u", [NFB, P, KC, 256])
    w2d = din("w2d", [KC, P, FC, P])
    win = din("win", [6, P, KC, 512])
    wout = din("wout", [2, P, KC, 512])
    wq = din("wq", [2, P, KC, 512])
    wk = din("wk", [2, P, KC, 512])
    wv = din("wv", [2, P, KC, 512])
    wo = din("wo", [2, P, KC, 512])
    outT = nc.dram_tensor("outT", [P, KC, T], F32, kind="ExternalOutput").ap()
    hsp = nc.dram_tensor("hsp", [NG, P, KC * GS], F32)
    hsb = nc.dram_tensor("hsb", [NG, P, KC * GS], BF16)
    osp = nc.dram_tensor("osp", [NG, P, 4 * GS], F32)
    qsp = nc.dram_tensor("qsp", [NG, P, 4 * GS], BF16)
    wcache = {nm: nc.dram_tensor("c_" + nm, shp, BF16) for nm, shp in (
        ("w1g", [NFB, P, KC * 256]), ("w1u", [NFB, P, KC * 256]), ("w1d", [KC, P, FC * P]),
        ("w2g", [NFB, P, KC * 256]), ("w2u", [NFB, P, KC * 256]), ("w2d", [KC, P, FC * P]))}
    cin = nc.dram_tensor("cin", [4 * P, P], F32)
    cout = nc.dram_tensor("cout", [8 * P, P], F32)

    with ExitStack() as st:
        S = Sched(nc, st)
        budget = [0]

        def sb(name, shape, dt):
            n = 1
            for s_ in shape[1:]:
                n *= s_
            budget[0] += n * (4 if dt == F32 else 2)
            return st.enter_context(nc.sbuf_tensor(name, list(shape), dt))

        H = sb("H", [P, KC, GS], F32)
        Hb2 = [sb("Hb_%d" % i, [P, KC, GS], BF16) for i in range(2)]
        OL = sb("OL", [P, 4, GS], F32)
        QP = sb("QP", [P, 4, GS], BF16)
        SCRB = sb("SCRB", [P, FC * GS], BF16)
        HGB = sb("HGB", [P, 19 * GS], BF16)
        TFR = sb("TFR", [P, 17 * GS], F32)
        mix = sb("mix", [P, KC, GS], BF16)
        WG = [sb("WG%d" % i, [P, KC, 256], BF16) for i in range(2)]
        WU = [sb("WU%d" % i, [P, KC, 256], BF16) for i in range(2)]
        WD = [sb("WD%d" % i, [P, FC, P], BF16) for i in range(2)]
        WB = [sb("WB%d" % i, [P, KC, 512], BF16) for i in range(2)]
        KT = sb("KT", [P, KC, MEM], BF16)
        Vm = sb("Vm", [P, 2, D], BF16)
        ident_f = sb("ident_f", [P, P], F32)
        ident_b = sb("ident_b", [P, P], BF16)
        onesK = sb("onesK", [P, P], F32)
        ones128 = sb("ones128", [P, P], F32)
        ones_b = sb("ones_b", [P, P], BF16)
        onesKb = sb("onesKb", [P, P], BF16)
        sqb = [sb("sqb%d" % i, [P, GS], BF16) for i in range(2)]
        MASK4 = sb("MASK4", [P, 4, P], BF16)
        rmask = sb("rmask", [P, GS], F32)
        WsT = sb("WsT", [P, 4, P], BF16)
        SGG = sb("SGG", [P, 512], F32)
        SGB = sb("SGB", [P, 512], F32)
        bs_hi = sb("bs_hi", [1, 512], BF16)
        bs_lo = sb("bs_lo", [1, 512], BF16)
        bs_lo2 = sb("bs_lo2", [1, 512], BF16)
        colt = sb("colt", [P, NCOLS], F32)
        colA = sb("colA", [P, 64], F32)
        lbc = sb("lbc", [P, 4], F32)
        omlc = sb("omlc", [P, 4], F32)
        flg = sb("flg", [P, 1], F32)
        Sst = sb("Sst", [P, 4, P], F32)
        Sbf = [[sb("Sbf%d_%d" % (h, i), [P, P], BF16) for i in range(2)] for h in range(4)]
        Sp = sb("Sp", [P, 4, P], F32)
        Spb = sb("Spb", [P, 4, P], BF16)
        LPX = sb("LPX", [P, 4, 9], F32)
        PXt = sb("PXt", [P, 4, 8], F32)
        EPS_AP = sb("epsc", [P, 1], F32)

        def tf(i, n=1):
            return TFR[:, i * GS:(i + n) * GS]
        tA, tB, tC, tD, tE, tF = (tf(i) for i in range(6))
        sl = [tf(6), tf(7)]
        E2 = [tf(8 + h) for h in range(4)]
        GU = TFR[:, 8 * GS:12 * GS].rearrange("p (g t) -> p g t", t=GS)
        lnT = [tf(12 + i) for i in range(5)]
        memf = TFR[:, 8 * GS:12 * GS].rearrange("p (c m) -> p c m", m=MEM)
        sgwt = TFR[:, 0:512].rearrange("p (g s) -> p g s", s=P)
        bsr = TFR[0:1, 2 * GS:3 * GS]
        bs_hf = TFR[0:1, 3 * GS:4 * GS]
        hmid = SCRB[:, :].rearrange("p (j t) -> p j t", t=GS)

        def bv(i, n=1):
            return SCRB[:, i * GS:(i + n) * GS]

        def hv(i, n=1):
            return HGB[:, i * GS:(i + n) * GS]
        qb = [hv(h) for h in range(4)]
        kdT = [hv(4 + h) for h in range(4)]
        scTs = [hv(8 + h).rearrange("p (i t) -> p i t", t=P) for h in range(4)]
        vt = hv(12, 4).rearrange("p (i f) -> p i f", f=512)
        qc, kc, kdec = hv(16), hv(17), hv(18)
        QT = bv(1, 8).rearrange("p (c t) -> p c t", t=GS)
        PT = [bv(9), bv(10)]
        memb = hv(0, 4).rearrange("p (c m) -> p c m", m=MEM)

        AL = S.alias
        for i_, n_ in enumerate(("tA", "tB", "tC", "tD", "tE", "tF", "sl0", "sl1")):
            AL[n_] = "tf%d" % i_
        for h_ in range(4):
            AL["E2_%d" % h_] = "tf%d" % (8 + h_)
            AL["GU%d" % h_] = "tf%d" % (8 + h_)
        for c_ in range(KC):
            AL["QT%d" % c_] = "scr%d" % (1 + c_)
        AL["PT0"], AL["PT1"] = "scr9", "scr10"
        AL["sgwt"], AL["bsr"], AL["bs_hf"] = "tf0", "tf2", "tf3"
        for j_ in range(FC):
            AL["hm%d" % j_] = "scr%d" % j_

        PS = [st.enter_context(nc.psum_tensor("PS%d" % i, [P, 512], F32)) for i in range(8)]
        pools = {"all": [0, 1, 2, 3, 4, 5], "A": [0, 1, 2, 3], "B": [4, 5]}
        rr = {"all": 0, "A": 0, "B": 0}
        ctx_ = {"pool": "all"}

        def bank():
            p_ = ctx_["pool"]
            i = pools[p_][rr[p_] % len(pools[p_])]
            rr[p_] += 1
            return PS[i], "PS%d" % i

        def CTX(tag, pool):
            S.tag = tag
            ctx_["pool"] = pool

        def MM(out, lhsT, rhs, start, stop, reads, writes):
            S.op("pe", lambda e: e.matmul(out, lhsT=lhsT, rhs=rhs, start=start, stop=stop),
                 reads=reads, writes=writes)

        def ACT(out, in_, func, reads, writes, scale=None, bias=None):
            kw = {}
            if scale is not None:
                kw["scale"] = scale
            if bias is not None:
                kw["bias"] = bias
            S.op("act", lambda e: e.activation(out=out, in_=in_, func=func, **kw), reads=reads, writes=writes)

        def TT(out, in0, in1, op, reads, writes, eng="dve"):
            S.op(eng, lambda e: e.tensor_tensor(out=out, in0=in0, in1=in1, op=op), reads=reads, writes=writes)

        def TS(out, in0, s1, s2, op0, op1, reads, writes, eng="dve"):
            if op1 is None:
                S.op(eng, lambda e: e.tensor_scalar(out=out, in0=in0, scalar1=s1, scalar2=None, op0=op0),
                     reads=reads, writes=writes)
            else:
                S.op(eng, lambda e: e.tensor_scalar(out=out, in0=in0, scalar1=s1, scalar2=s2, op0=op0, op1=op1),
                     reads=reads, writes=writes)

        def STT(out, in0, scalar, in1, op0, op1, reads, writes):
            S.op("dve", lambda e: e.scalar_tensor_tensor(out=out, in0=in0, scalar=scalar, in1=in1, op0=op0, op1=op1),
                 reads=reads, writes=writes)

        def DMA(eng, out, in_, sem, reads, writes):
            S.dma(eng, lambda e: e.dma_start(out=out, in_=in_), sem, reads=reads, writes=writes)

        def run(gen):
            for _ in gen:
                pass

        def interleave(ga, na, gb, nb):
            ia = ib = 0
            da = db = False
            while not (da and db):
                if not da and (db or ia * nb <= ib * na):
                    try:
                        next(ga)
                        ia += 1
                    except StopIteration:
                        da = True
                else:
                    try:
                        next(gb)
                        ib += 1
                    except StopIteration:
                        db = True

        Hk = ["H%d" % c for c in range(KC)]
        Hbk2 = [["Hb%d_%d" % (i, c) for c in range(KC)] for i in range(2)]
        mixk = ["mix%d" % c for c in range(KC)]

        setup_keys = ["colt", "SGG", "SGB", "bsr", "flg", "sgwt", "E2_0", "E2_1", "E2_2", "E2_3"]
        DMA("sp", colt[:], cols, "SETUP", [], ["colt"])
        DMA("sp", SGG[:], sgrow[0, :].partition_broadcast(P), "SETUP", [], ["SGG"])
        DMA("sp", SGB[:], sgrow[1, :].partition_broadcast(P), "SETUP", [], ["SGB"])
        DMA("sp", bsr, sgrow[2:3, :], "SETUP", [], ["bsr"])
        DMA("sp", flg[:], flag, "SETUP", [], ["flg"])
        DMA("sp", sgwt, sgw, "SETUP", [], ["sgwt"])
        DMA("sp", memf, memT, "SETUP", [], ["E2_0", "E2_1", "E2_2", "E2_3"])
        S.fence(setup_keys, "SETUP")

        S.op("pool", lambda e: e.memset(EPS_AP[:], EPS), writes=["epsc"])
        S.op("pool", lambda e: e.memset(ident_f[:], 0.0), writes=["ident_f"])
        S.op("pool", lambda e: e.affine_select(out=ident_f[:], in_=ident_f[:], pattern=[[-1, P]],
                                               compare_op=ALU.not_equal, fill=1.0, base=0, channel_multiplier=1),
             reads=["ident_f"], writes=["ident_f"])
        S.op("dve", lambda e: e.tensor_copy(out=ident_b[:], in_=ident_f[:]), reads=["ident_f"], writes=["ident_b"])
        S.op("pool", lambda e: e.memset(onesK[:], 1.0 / D), writes=["onesK"])
        S.op("pool", lambda e: e.memset(ones128[:], 1.0 / P), writes=["ones128"])
        S.op("pool", lambda e: e.memset(ones_b[:], 1.0), writes=["ones_b"])
        S.op("pool", lambda e: e.memset(onesKb[:], 1.0 / D), writes=["onesKb"])
        S.op("pool", lambda e: e.memset(rmask[:], 1.0), writes=["rmask"])
        S.op("pool", lambda e: e.memset(rmask[:].rearrange("p (c t) -> p c t", t=64)[:, :, 0:1], 0.0),
             reads=["rmask"], writes=["rmask"])
        S.op("pool", lambda e: e.memset(tB[:, 0:P], 1.0), writes=["tB"])
        S.op("pool", lambda e: e.affine_select(out=tB[:, 0:P], in_=tB[:, 0:P], pattern=[[1, P]],
                                               compare_op=ALU.is_ge, fill=0.0, base=0, channel_multiplier=-1),
             reads=["tB"], writes=["tB"])
        S.op("pool", lambda e: e.memset(tB[0:64, 64:P], 0.0), reads=["tB"], writes=["tB"])
        for i in range(4):
            S.op("dve", (lambda i: lambda e: e.tensor_copy(out=MASK4[:, i, :], in_=tB[:, 0:P]))(i),
                 reads=["tB"], writes=["MASK4"])
        S.op("pool", lambda e: e.memset(Sst[:], 0.0), writes=["S0", "S1", "S2", "S3"])
        for h in range(4):
            S.op("pool", (lambda h: lambda e: e.memset(Sbf[h][0][:], 0.0))(h), writes=["Sbf%d_0" % h])
        S.op("pool", lambda e: e.memset(LPX[:], 0.0), writes=["LPX0", "LPX1", "LPX2", "LPX3"])
        TS(colA[:], colt[:, 0:64], ALPHA, None, ALU.mult, None, ["colt"], ["colA"])
        TT(lbc[:], colt[:, C_LB0:C_LB0 + 4], colt[:, C_LB1:C_LB1 + 4], ALU.subtract, ["colt"], ["lbc"])
        ACT(lbc[:], lbc[:], AF.Sigmoid, ["lbc"], ["lbc"])
        TS(omlc[:], lbc[:], -1.0, 1.0, ALU.mult, ALU.add, ["lbc"], ["omlc"])
        S.op("dve", lambda e: e.tensor_copy(out=bs_hi[:], in_=bsr), reads=["bsr"], writes=["bs_hi"])
        S.op("dve", lambda e: e.tensor_copy(out=bs_hf, in_=bs_hi[:]), reads=["bs_hi"], writes=["bs_hf"])
        TT(bsr, bsr, bs_hf, ALU.subtract, ["bsr", "bs_hf"], ["bsr"])
        S.op("dve", lambda e: e.tensor_copy(out=bs_lo[:], in_=bsr), reads=["bsr"], writes=["bs_lo"])
        S.op("dve", lambda e: e.tensor_copy(out=bs_hf, in_=bs_lo[:]), reads=["bs_lo"], writes=["bs_hf"])
        TT(bs_lo2[:], bsr, bs_hf, ALU.subtract, ["bsr", "bs_hf"], ["bs_lo2"])
        for g_ in range(4):
            S.op("pool", (lambda g_: lambda e: e.affine_select(out=sgwt[:, g_, :], in_=sgwt[:, g_, :], pattern=[[-1, P]],
                                                               compare_op=ALU.is_ge, fill=0.0, base=0, channel_multiplier=1))(g_),
                 reads=["sgwt"], writes=["sgwt"])
        pb, pk = bank()
        for g_ in range(4):
            S.op("pe", (lambda g_: lambda e: e.transpose(pb[:, g_ * P:(g_ + 1) * P], in_=sgwt[:, g_, :], identity=ident_f[:]))(g_),
                 reads=["sgwt", "ident_f"], writes=[pk])
        S.op("dve", lambda e: e.tensor_copy(out=WsT[:], in_=pb[:, :].rearrange("p (g t) -> p g t", t=P)),
             reads=[pk], writes=["WsT"])

        def layer_norm(src, srck, N, gcol, bcol, tag, pool, out_f=None, out_fk=None, agcol=None, abcol=None,
                       out_b=None, out_bk=None):
            CTX(tag, pool)
            p1, k1 = bank()
            for c in range(KC):
                MM(p1[:, 0:N], onesK[:], src[:, c, :], c == 0, c == KC - 1, [srck[c], "onesK"], [k1])
            p2, k2 = bank()
            for c in range(KC):
                sq, sqk = sqb[c % 2][:, 0:N], "sqb%d" % (c % 2)
                ACT(sq, src[:, c, :], AF.Square, [srck[c]], [sqk])
                MM(p2[:, 0:N], onesKb[:], sq, c == 0, c == KC - 1, [sqk, "onesKb"], [k2])
            yield
            CTX(tag, pool)
            mean, rs_, nm_ = lnT[0][:, 0:N], lnT[1][:, 0:N], lnT[2][:, 0:N]
            ACT(mean, p1[:, 0:N], AF.Copy, [k1], ["lnA"])
            TT(rs_, mean, mean, ALU.mult, ["lnA"], ["lnB"])
            TT(rs_, p2[:, 0:N], rs_, ALU.subtract, [k2, "lnB"], ["lnB"])
            ACT(rs_, rs_, AF.Ln, ["lnB", "epsc"], ["lnB"], bias=EPS_AP[:, 0:1])
            ACT(rs_, rs_, AF.Exp, ["lnB"], ["lnB"], scale=-0.5)
            STT(nm_, mean, -1.0, rs_, ALU.mult, ALU.mult, ["lnA", "lnB"], ["lnC"])
            for c in range(KC):
                tmp, tk = (lnT[3], "lnD") if c % 2 == 0 else (lnT[4], "lnE")
                tmp = tmp[:, 0:N]
                TT(tmp, src[:, c, :], rs_, ALU.mult, [srck[c], "lnB"], [tk])
                TT(tmp, tmp, nm_, ALU.add, [tk, "lnC"], [tk])
                if out_b is not None:
                    ACT(out_b[:, c, :], tmp, AF.Identity, [tk, "colt"], [out_bk[c]],
                        scale=colt[:, gcol + c:gcol + c + 1], bias=colt[:, bcol + c:bcol + c + 1])
                if out_f is not None:
                    if agcol is not None:
                        sc_, bi_ = colA[:, agcol + c:agcol + c + 1], colA[:, abcol + c:abcol + c + 1]
                    else:
                        sc_, bi_ = colt[:, gcol + c:gcol + c + 1], colt[:, bcol + c:bcol + c + 1]
                    ACT(out_f[:, c, :], tmp, AF.Identity, [tk, "colt", "colA"], [out_fk[c]], scale=sc_, bias=bi_)
                if c % 4 == 3:
                    yield
                    CTX(tag, pool)

        memk = ["E2_%d" % (c // 2) for c in range(KC)]
        membk = ["qb%d" % (c // 2) for c in range(KC)]
        wload_ctr = [0]

        def load_wb(src_ap):
            i = wload_ctr[0] % 2
            wload_ctr[0] += 1
            DMA("pool", WB[i][:], src_ap, "WB%d" % i, [], ["WB%d" % i])
            return WB[i], "WB%d" % i

        def gen_setup_kv(pool):
            yield from layer_norm(memf, memk, MEM, C_LN["m"][0], C_LN["m"][1], "setup", pool, out_b=memb, out_bk=membk)
            for blk in range(2):
                CTX("setup", pool)
                wt, wkk = load_wb(wk[blk])
                for cc in range(4):
                    CTX("setup", pool)
                    fc_ = blk * 4 + cc
                    pb, pk = bank()
                    for k in range(KC):
                        MM(pb[:, 0:MEM], wt[:, k, cc * P:(cc + 1) * P], memb[:, k, :], k == 0, k == KC - 1, [wkk, membk[k]], [pk])
                    ACT(KT[:, fc_, :], pb[:, 0:MEM], AF.Copy, [pk], ["KT"])
                    yield
            for blk in range(2):
                CTX("setup", pool)
                wt, wkk = load_wb(wv[blk])
                for mt in range(2):
                    CTX("setup", pool)
                    pb, pk = bank()
                    for k in range(KC):
                        MM(pb[:, :], memb[:, k, mt * P:(mt + 1) * P], wt[:, k, :], k == 0, k == KC - 1, [wkk, membk[k]], [pk])
                    ACT(Vm[:, mt, blk * 512:(blk + 1) * 512], pb[:, :], AF.Copy, [pk], ["Vm"])
                    yield

        gu_ctr = [0]
        wd_ctr = [0]

        def wload(buf, bufk, src_d, nm, idx, first, pat, n_):
            ck = "wc_%s_%d" % (nm, idx)
            cview = wcache[nm][idx].rearrange(pat, **{pat.split("(")[1].split(")")[0].split()[1]: n_})
            if not USE_WCACHE:
                DMA("pool", buf[:], src_d[idx], bufk, [], [bufk])
            elif first:
                DMA("pool", buf[:], src_d[idx], bufk, [], [bufk])
                DMA("sp", cview, buf[:], "CW_" + bufk, [bufk], [ck])
            else:
                DMA("sp", buf[:], cview, bufk, [ck], [bufk])

        def ffn(wg_d, wu_d, wd_d, Hb, Hbk, pool, nm=None, first=True):
            yield from ffn_gu(wg_d, wu_d, Hb, Hbk, pool, nm, first)
            yield from ffn_down(wd_d, pool, nm, first)

        def ffn_gu(wg_d, wu_d, Hb, Hbk, pool, nm=None, first=True):
            for jb in range(NFB):
                CTX("ffn_gu", pool)
                b_ = gu_ctr[0] % 2
                gu_ctr[0] += 1
                wload(WG[b_], "WG%d" % b_, wg_d, nm + "g", jb, first, "p (c n) -> p c n", 256)
                wload(WU[b_], "WU%d" % b_, wu_d, nm + "u", jb, first, "p (c n) -> p c n", 256)
                for jj in range(2):
                    j = jb * 2 + jj
                    pg, pgk = bank()
                    for k in range(KC):
                        MM(pg[:, :], WG[b_][:, k, jj * P:(jj + 1) * P], Hb[:, k, :], k == 0, k == KC - 1,
                           ["WG%d" % b_, Hbk[k]], [pgk])
                    pu, puk = bank()
                    for k in range(KC):
                        MM(pu[:, :], WU[b_][:, k, jj * P:(jj + 1) * P], Hb[:, k, :], k == 0, k == KC - 1,
                           ["WU%d" % b_, Hbk[k]], [puk])
                    s_, sk_ = sl[j % 2], "sl%d" % (j % 2)
                    ACT(s_, pg[:, :], AF.Silu, [pgk], [sk_])
                    TT(hmid[:, j, :], s_, pu[:, :], ALU.mult, [sk_, puk], ["hm%d" % j])
                    yield
                    CTX("ffn_gu", pool)

        def ffn_down(wd_d, pool, nm=None, first=True):
            for c in range(KC):
                CTX("ffn_down", pool)
                b_ = wd_ctr[0] % 2
                wd_ctr[0] += 1
                wload(WD[b_], "WD%d" % b_, wd_d, nm + "d", c, first, "p (j n) -> p j n", P)
                po, pok = bank()
                for j in range(FC):
                    MM(po[:, :], WD[b_][:, j, :], hmid[:, j, :], j == 0, j == FC - 1, ["WD%d" % b_, "hm%d" % j], [pok])
                STT(H[:, c, :], po[:, :], 0.5, H[:, c, :], ALU.mult, ALU.add, [pok, Hk[c]], [Hk[c]])
                yield

        ln1_done = [False] * NG

        def gen_gu1(g, pool):
            gsl = slice(g * GS, (g + 1) * GS)
            Hb, Hbk = Hb2[g % 2], Hbk2[g % 2]
            CTX("xload", pool)
            DMA("pool", Hb[:], xT[:, :, gsl], "LXB_%d" % g, [], Hbk)
            yield
            yield from ffn_gu(w1g, w1u, Hb, Hbk, pool, "w1", g == 0)

        def gen_down1(g, pool):
            gsl = slice(g * GS, (g + 1) * GS)
            assert g == 0 or ln1_done[g - 1], "H still owned by the previous group's LayerNorm"
            CTX("xload", pool)
            DMA("sp", H[:], xT[:, :, gsl], "LHx_%d" % g, [], Hk)
            ACT(H[:], H[:], AF.Identity, Hk, Hk, scale=ALPHA)
            yield
            yield from ffn_down(w1d, pool, "w1", g == 0)

        def gen_ln1(g, pool):
            gsl = slice(g * GS, (g + 1) * GS)
            Hb, Hbk = Hb2[g % 2], Hbk2[g % 2]
            if stage == 1:
                yield from layer_norm(H, Hk, GS, C_LN[1][0], C_LN[1][1], "ln", pool, out_f=H, out_fk=Hk)
                DMA("sp", outT[:, :, gsl], H[:], "OUT_%d" % g, Hk, ["out%d" % g])
                ln1_done[g] = True
                return
            yield from layer_norm(H, Hk, GS, C_LN[1][0], C_LN[1][1], "ln", pool, out_f=H, out_fk=Hk,
                                  agcol=C_LN[1][0], abcol=C_LN[1][1], out_b=Hb, out_bk=Hbk)
            CTX("spill", pool)
            DMA("sp", hsp[g].rearrange("p (c t) -> p c t", t=GS), H[:], "SPL_%d" % g, Hk, ["hsp%d" % g])
            DMA("sp", hsb[g].rearrange("p (c t) -> p c t", t=GS), Hb[:], "SPL2_%d" % g, Hbk, ["hsb%d" % g])
            ln1_done[g] = True
            yield

        def chain(*gens):
            for g_ in gens:
                yield from g_

        cur = [0, 0, 0, 0]

        def gen_hg(g, pool):
            Hb, Hbk = Hb2[g % 2], Hbk2[g % 2]
            CTX("hg_prep", pool)
            wi_, wik = load_wb(win[2])
            for i in range(4):
                CTX("hg_prep", pool)
                pb, pk = bank()
                for k in range(KC):
                    MM(pb[:, :], Hb[:, k, i * P:(i + 1) * P], wi_[:, k, :], k == 0, k == KC - 1, [Hbk[k], wik], [pk])
                ACT(vt[:, i, :], pb[:, :], AF.Copy, [pk], ["vt%d" % i])
                yield
            CTX("hg_prep", pool)
            wq_, wqk = load_wb(win[0])
            wf_, wfk = load_wb(win[1])
            for h in range(4):
                CTX("hg_prep", pool)
                hs = slice(h * P, (h + 1) * P)
                pq, pqk = bank()
                for k in range(KC):
                    MM(pq[:, :], wq_[:, k, hs], Hb[:, k, :], k == 0, k == KC - 1, [wqk, Hbk[k]], [pqk])
                pf, pfk = bank()
                for k in range(KC):
                    MM(pf[:, :], wf_[:, k, hs], Hb[:, k, :], k == 0, k == KC - 1, [wfk, Hbk[k]], [pfk])
                ACT(tA, pf[:, :], AF.Sigmoid, [pfk], ["tA"])
                TS(tA, tA, omlc[:, h:h + 1], lbc[:, h:h + 1], ALU.mult, ALU.add, ["tA", "omlc", "lbc"], ["tA"])
                yield
                CTX("hg_prep", pool)
                ACT(tB, tA, AF.Ln, ["tA"], ["tB"])
                TS(tA, tA, -1.0, 1.0, ALU.mult, ALU.add, ["tA"], ["tA"])
                S.op("dve", lambda e: e.tensor_tensor_scan(out=tC, data0=rmask[:], data1=tB, initial=0.0,
                                                           op0=ALU.mult, op1=ALU.add),
                     reads=["rmask", "tB"], writes=["tC"])
                yield
                CTX("hg_prep", pool)
                b3 = tC.rearrange("p (c t) -> p c t", t=64)
                TT(tD.rearrange("p (c t) -> p c t", t=64), b3, b3[:, :, 31:32].to_broadcast([P, 8, 64]), ALU.subtract,
                   ["tC"], ["tD"])
                TT(tE.rearrange("p (c t) -> p c t", t=64), b3[:, :, 63:64].to_broadcast([P, 8, 64]), b3, ALU.subtract,
                   ["tC"], ["tE"])
                ACT(tB, tD, AF.Exp, ["tD"], ["tB"])
                TT(qc, pq[:, :], tB, ALU.mult, [pqk, "tB"], ["qc"])
                yield
                CTX("hg_prep", pool)
                ACT(tF, tD, AF.Exp, ["tD"], ["tF"], scale=-1.0)
                TT(kc, tA, tF, ALU.mult, ["tA", "tF"], ["kc"])
                yield
                CTX("hg_prep", pool)
                ACT(E2[h], tC, AF.Exp, ["tC"], ["E2_%d" % h])
                TT(qb[h], pq[:, :], E2[h], ALU.mult, [pqk, "E2_%d" % h], ["qb%d" % h])
                yield
                CTX("hg_prep", pool)
                ACT(tE, tE, AF.Exp, ["tE"], ["tE"])
                TT(kdec, tA, tE, ALU.mult, ["tA", "tE"], ["kdec"])
                yield
                CTX("hg_prep", pool)
                S.op("dve", (lambda h: lambda e: e.tensor_tensor_scan(
                    out=LPX[:, h, 1:9], data0=rmask[:, 1:9], data1=tC.rearrange("p (c t) -> p c t", t=64)[:, :, 63],
                    initial=LPX[:, h, 0:1], op0=ALU.mult, op1=ALU.add))(h),
                    reads=["rmask", "tC", "LPX%d" % h], writes=["LPX%d" % h])
                ACT(PXt[:, h, :], LPX[:, h, 0:8], AF.Exp, ["LPX%d" % h], ["PXt%d" % h])
                TT(QP[:, h, :].rearrange("p (c t) -> p c t", t=64), qb[h].rearrange("p (c t) -> p c t", t=64),
                   PXt[:, h, :].unsqueeze(2).to_broadcast([P, 8, 64]), ALU.mult,
                   ["qb%d" % h, "PXt%d" % h], ["QP%d" % h])
                S.op("dve", (lambda h: lambda e: e.tensor_copy(out=LPX[:, h, 0:1], in_=LPX[:, h, 8:9]))(h),
                     reads=["LPX%d" % h], writes=["LPX%d" % h])
                yield
                CTX("hg_prep", pool)
                pt, ptk = bank()
                ptb = pt[:, :].bitcast(BF16)
                for i in range(4):
                    S.op("pe", (lambda i, ptb: lambda e: e.transpose(ptb[:, i * P:(i + 1) * P], in_=kdec[:, i * P:(i + 1) * P],
                                                                      identity=ident_b[:]))(i, ptb),
                         reads=["kdec", "ident_b"], writes=[ptk])
                ACT(kdT[h], ptb[:, 0:GS], AF.Copy, [ptk], ["kdT%d" % h])
                yield
                CTX("hg_prep", pool)
                psc, psck = bank()
                for i in range(4):
                    ts_ = slice(i * P, (i + 1) * P)
                    MM(psc[:, ts_], kc[:, ts_], qc[:, ts_], True, True, ["kc", "qc"], [psck])
                TT(scTs[h], psc[:, :].rearrange("p (i t) -> p i t", t=P), MASK4[:], ALU.mult, [psck, "MASK4"], ["scTs%d" % h])
                yield
            for i in range(4):
                CTX("hg_scan", pool)
                po, pok = bank()
                for half in range(2):
                    c = 2 * i + half
                    gc = g * 8 + c
                    rows = slice(half * 64, half * 64 + 64)
                    cols_ = slice(i * P + half * 64, i * P + half * 64 + 64)
                    for h in range(4):
                        hs = slice(h * P, (h + 1) * P)
                        slot = PS[6 + gc % 2][:, hs]
                        MM(slot, kdT[h][rows, i * P:(i + 1) * P], vt[rows, i, hs], True, True,
                           ["kdT%d" % h, "vt%d" % i], ["PSd%d" % (gc % 2)])
                    for h in range(4):
                        hs = slice(h * P, (h + 1) * P)
                        ocol = slice(h * P + half * 64, h * P + half * 64 + 64)
                        MM(po[:, ocol], vt[rows, i, hs], scTs[h][rows, i, half * 64:half * 64 + 64], True, False,
                           ["vt%d" % i, "scTs%d" % h], [pok])
                        MM(po[:, ocol], Sbf[h][cur[h]][:], qb[h][:, cols_], False, True,
                           ["Sbf%d_%d" % (h, cur[h]), "qb%d" % h], [pok])
                    for h in range(4):
                        slot = PS[6 + gc % 2][:, h * P:(h + 1) * P]
                        STT(Sst[:, h, :], Sst[:, h, :], E2[h][:, cols_.stop - 1:cols_.stop], slot, ALU.mult, ALU.add,
                            ["S%d" % h, "E2_%d" % h, "PSd%d" % (gc % 2)], ["S%d" % h])
                        ACT(Sbf[h][1 - cur[h]][:], Sst[:, h, :], AF.Copy, ["S%d" % h], ["Sbf%d_%d" % (h, 1 - cur[h])])
                        cur[h] = 1 - cur[h]
                ACT(OL[:, :, i * P:(i + 1) * P], po[:, :].rearrange("p (h t) -> p h t", t=P), AF.Copy,
                    [pok], ["OL%d" % i])
                yield
            CTX("spill", pool)
            DMA("sp", osp[g].rearrange("p (h t) -> p h t", t=GS), OL[:], "SPL3_%d" % g, ["OL%d" % i for i in range(4)], ["osp%d" % g])
            DMA("sp", qsp[g].rearrange("p (h t) -> p h t", t=GS), QP[:], "SPL4_%d" % g, ["QP%d" % h for h in range(4)], ["qsp%d" % g])
            yield

        vln4_std = hv(0, 4).rearrange("p (i f) -> p i f", f=512)
        vln4_alt = bv(13, 4).rearrange("p (i f) -> p i f", f=512)
        bst16 = sb("bst16", [P, 16, 6], F32)
        bmv16 = sb("bmv16", [P, 16, 2], F32)
        r16 = sb("r16", [P, 16], F32)
        n16 = sb("n16", [P, 16], F32)
        gV = gV_std = [tA, tB, tC, tD]
        gVk = gVk_std = ["tA", "tB", "tC", "tD"]
        sqs = [(tE, "tE"), (tF, "tF"), (sl[0], "sl0"), (sl[1], "sl1")]

        GUalt = TFR[:, 12 * GS:16 * GS].rearrange("p (g t) -> p g t", t=GS)
        GUaltk = ["lnA", "lnB", "lnC", "lnD"]
        GUstd = (GU, ["GU%d" % gg for gg in range(4)])

        def gen_mixer(g, pool, gu=None, delay=0):
            yield from gen_mixer_sgu(g, pool, gu, delay)
            yield from gen_mixer_fin(g, pool, gu)

        def gen_mixer_sgu(g, pool, gu=None, delay=0, vl=None, altw=False, gv=None):
            for _ in range(delay):
                yield
            GU, GUk = gu if gu is not None else GUstd
            gV, gVk = gv if gv is not None else (gV_std, gVk_std)
            vln4, vlk = vl if vl is not None else (vln4_std, ["qb%d" % i for i in range(4)])
            Hb, Hbk = Hb2[g % 2], Hbk2[g % 2]
            CTX("mix_load", pool)
            DMA("sp", Hb[:], hsb[g].rearrange("p (c t) -> p c t", t=GS), "LHB_%d" % g, ["hsb%d" % g], Hbk)
            yield
            CTX("sgu", pool)
            if altw:
                for hh in range(2):
                    DMA("pool", WG[hh][:], win[4][:, :, hh * 256:(hh + 1) * 256], "WG%d" % hh, [], ["WG%d" % hh])
            else:
                wu_, wuk = load_wb(win[4])
            for gg in range(4):
                CTX("sgu", pool)
                pu, puk = bank()
                for k in range(KC):
                    if altw:
                        lw, lwk = WG[gg // 2][:, k, (gg % 2) * P:(gg % 2 + 1) * P], "WG%d" % (gg // 2)
                    else:
                        lw, lwk = wu_[:, k, gg * P:(gg + 1) * P], wuk
                    MM(pu[:, :], lw, Hb[:, k, :], k == 0, k == KC - 1, [lwk, Hbk[k]], [puk])
                ACT(GU[:, gg, :], pu[:, :], AF.Gelu_apprx_tanh, [puk], [GUk[gg]])
                yield
            CTX("sgu", pool)
            if altw:
                for hh in range(2):
                    DMA("pool", WU[hh][:], win[5][:, :, hh * 256:(hh + 1) * 256], "WU%d" % hh, [], ["WU%d" % hh])
            else:
                wv_, wvk = load_wb(win[5])
            for i in range(4):
                CTX("sgu", pool)
                ts_ = slice(i * P, (i + 1) * P)
                pv, pvk = bank()
                if altw:
                    for hh in range(2):
                        for k in range(KC):
                            MM(pv[:, hh * 256:(hh + 1) * 256], Hb[:, k, ts_], WU[hh][:, k, :], k == 0, k == KC - 1,
                               [Hbk[k], "WU%d" % hh], [pvk])
                else:
                    for k in range(KC):
                        MM(pv[:, :], Hb[:, k, ts_], wv_[:, k, :], k == 0, k == KC - 1, [Hbk[k], wvk], [pvk])
                ACT(gV[i], pv[:, :], AF.Gelu_apprx_tanh, [pvk], [gVk[i]])
                for gg in range(4):
                    S.op("dve", (lambda i, gg: lambda e: e.bn_stats(out=bst16[:, i * 4 + gg, :], in_=gV[i][:, gg * P:(gg + 1) * P]))(i, gg),
                         reads=[gVk[i]], writes=["bst16"])
                for gg in range(4):
                    S.op("dve", (lambda i, gg: lambda e: e.bn_aggr(out=bmv16[:, i * 4 + gg, :], in_=bst16[:, i * 4 + gg, :]))(i, gg),
                         reads=["bst16"], writes=["bmv16"])
                yield
            CTX("sgu", pool)
            ACT(r16[:], bmv16[:, :, 1], AF.Ln, ["bmv16", "epsc"], ["r16"], bias=EPS_AP[:, 0:1])
            ACT(r16[:], r16[:], AF.Exp, ["r16"], ["r16"], scale=-0.5)
            STT(n16[:], bmv16[:, :, 0], -1.0, r16[:], ALU.mult, ALU.mult, ["bmv16", "r16"], ["n16"])
            for i in range(4):
                CTX("sgu", pool)
                for gg in range(4):
                    j_ = i * 4 + gg
                    ACT(gV[i][:, gg * P:(gg + 1) * P], gV[i][:, gg * P:(gg + 1) * P], AF.Identity, [gVk[i], "r16", "n16"], [gVk[i]],
                        scale=r16[:, j_:j_ + 1], bias=n16[:, j_:j_ + 1])
                TT(gV[i], gV[i], SGG[:], ALU.mult, [gVk[i], "SGG"], [gVk[i]])
                TT(vln4[:, i, :], gV[i], SGB[:], ALU.add, [gVk[i], "SGB"], [vlk[i]])
                yield
            for i in range(4):
                CTX("sgu", pool)
                ts_ = slice(i * P, (i + 1) * P)
                ps_, psk = bank()
                for gg in range(4):
                    gs_ = slice(gg * P, (gg + 1) * P)
                    MM(ps_[:, gs_], vln4[:, i, gs_], WsT[:, gg, :], True, False, [vlk[i], "WsT"], [psk])
                    MM(ps_[:, gs_], ones_b[0:1, :], bs_hi[0:1, gs_], False, False, ["ones_b", "bs_hi"], [psk])
                    MM(ps_[:, gs_], ones_b[0:1, :], bs_lo[0:1, gs_], False, False, ["ones_b", "bs_lo"], [psk])
                    MM(ps_[:, gs_], ones_b[0:1, :], bs_lo2[0:1, gs_], False, True, ["ones_b", "bs_lo2"], [psk])
                TT(mix[:, 4:8, ts_], ps_[:, :].rearrange("p (g t) -> p g t", t=P), GU[:, :, ts_], ALU.mult,
                   [psk] + list(GUk), mixk[4:8])
                yield

        def gen_mixer_fin(g, pool, gu=None):
            GU, GUk = gu if gu is not None else GUstd
            Hb, Hbk = Hb2[g % 2], Hbk2[g % 2]
            CTX("hg_fin", pool)
            DMA("sp", OL[:], osp[g].rearrange("p (h t) -> p h t", t=GS), "LOL_%d" % g, ["osp%d" % g], ["OL%d" % i for i in range(4)])
            DMA("sp", QP[:], qsp[g].rearrange("p (h t) -> p h t", t=GS), "LQP_%d" % g, ["qsp%d" % g], ["QP%d" % h for h in range(4)])
            wg_, wgk = load_wb(win[3])
            for h in range(4):
                CTX("hg_fin", pool)
                pc, pck = bank()
                MM(pc[:, :], Spb[:, h, :], QP[:, h, :], True, True, ["Spb", "QP%d" % h], [pck])
                TT(gV[h], OL[:, h, :], pc[:, :], ALU.add, ["OL%d" % i for i in range(4)] + [pck], [gVk[h]])
                ACT(sqs[h][0], gV[h], AF.Square, [gVk[h]], [sqs[h][1]])
                yield
            for h in range(4):
                CTX("hg_fin", pool)
                pm, pmk = bank()
                MM(pm[:, :], ones128[:], sqs[h][0], True, True, ["ones128", sqs[h][1]], [pmk])
                ACT(sqs[h][0], pm[:, :], AF.Ln, [pmk, "epsc"], [sqs[h][1]], bias=EPS_AP[:, 0:1])
                yield
            for h in range(4):
                CTX("hg_fin", pool)
                ACT(sqs[h][0], sqs[h][0], AF.Exp, [sqs[h][1]], [sqs[h][1]], scale=-0.5)
                TT(gV[h], gV[h], sqs[h][0], ALU.mult, [gVk[h], sqs[h][1]], [gVk[h]])
            yield
            for h in range(4):
                CTX("hg_fin", pool)
                hs = slice(h * P, (h + 1) * P)
                pg, pgk = bank()
                for k in range(KC):
                    MM(pg[:, :], wg_[:, k, hs], Hb[:, k, :], k == 0, k == KC - 1, [wgk, Hbk[k]], [pgk])
                ACT(GU[:, h, :], pg[:, :], AF.Silu, [pgk], [GUk[h]])
                STT(mix[:, h, :], gV[h], colt[:, C_HGN:C_HGN + 1], GU[:, h, :], ALU.mult, ALU.mult,
                    [gVk[h], GUk[h], "colt"], [mixk[h]])
                yield

        def gen_rest(g, pool):
            gsl = slice(g * GS, (g + 1) * GS)
            Hb, Hbk = Hb2[g % 2], Hbk2[g % 2]
            CTX("w_out", pool)
            DMA("sp", H[:], hsp[g].rearrange("p (c t) -> p c t", t=GS), "LHs_%d" % g, ["hsp%d" % g], Hk)
            if stage == 20:
                ACT(H[:], mix[:], AF.Copy, mixk, Hk)
                DMA("sp", outT[:, :, gsl], H[:], "OUT_%d" % g, Hk, ["out%d" % g])
                return
            wo0, wo0k = load_wb(wout[0])
            wo1, wo1k = load_wb(wout[1])
            for c in range(KC):
                wt, wtk = (wo0, wo0k) if c < 4 else (wo1, wo1k)
                cs = slice((c % 4) * P, (c % 4 + 1) * P)
                po, pok = bank()
                for k in range(KC):
                    MM(po[:, :], wt[:, k, cs], mix[:, k, :], k == 0, k == KC - 1, [wtk, mixk[k]], [pok])
                TT(H[:, c, :], po[:, :], H[:, c, :], ALU.add, [pok, Hk[c]], [Hk[c]])
            yield
            if stage == 2:
                yield from layer_norm(H, Hk, GS, C_LN[2][0], C_LN[2][1], "ln", pool, out_f=H, out_fk=Hk)
                DMA("sp", outT[:, :, gsl], H[:], "OUT_%d" % g, Hk, ["out%d" % g])
                return
            yield from layer_norm(H, Hk, GS, C_LN[2][0], C_LN[2][1], "ln", pool, out_f=H, out_fk=Hk,
                                  agcol=C_LN[2][0], abcol=C_LN[2][1], out_b=Hb, out_bk=Hbk)
            CTX("xattn", pool)
            wq0, wq0k = load_wb(wq[0])
            wq1, wq1k = load_wb(wq[1])
            for fc_ in range(KC):
                wt, wtk = (wq0, wq0k) if fc_ < 4 else (wq1, wq1k)
                cs = slice((fc_ % 4) * P, (fc_ % 4 + 1) * P)
                pq, pqk = bank()
                for k in range(KC):
                    MM(pq[:, :], wt[:, k, cs], Hb[:, k, :], k == 0, k == KC - 1, [wtk, Hbk[k]], [pqk])
                ACT(QT[:, fc_, :], pq[:, :], AF.Copy, [pqk], ["QT%d" % fc_])
            yield
            CTX("xattn", pool)
            for h in range(4):
                for mt in range(2):
                    psc, psck = bank()
                    for ec in range(2):
                        MM(psc[:, :], KT[:, h * 2 + ec, mt * P:(mt + 1) * P], QT[:, h * 2 + ec, :], ec == 0, ec == 1,
                           ["KT", "QT%d" % (h * 2 + ec)], [psck])
                    ACT(PT[mt], psc[:, :], AF.Exp, [psck], ["PT%d" % mt], scale=1.0 / 16.0)
                psm, psmk = bank()
                for mt in range(2):
                    MM(psm[:, :], ones_b[:], PT[mt], mt == 0, mt == 1, ["ones_b", "PT%d" % mt], [psmk])
                ACT(tA, psm[:, :], AF.Ln, [psmk], ["tA"])
                ACT(tA, tA, AF.Exp, ["tA"], ["tA"], scale=-1.0)
                for ec in range(2):
                    pv, pvk = bank()
                    for mt in range(2):
                        MM(pv[:, :], Vm[:, mt, (h * 2 + ec) * P:(h * 2 + ec + 1) * P], PT[mt], mt == 0, mt == 1,
                           ["Vm", "PT%d" % mt], [pvk])
                    TT(mix[:, h * 2 + ec, :], pv[:, :], tA, ALU.mult, [pvk, "tA"], [mixk[h * 2 + ec]])
            yield
            CTX("xattn", pool)
            wo0, wo0k = load_wb(wo[0])
            wo1, wo1k = load_wb(wo[1])
            for c in range(KC):
                wt, wtk = (wo0, wo0k) if c < 4 else (wo1, wo1k)
                cs = slice((c % 4) * P, (c % 4 + 1) * P)
                po, pok = bank()
                for k in range(KC):
                    MM(po[:, :], wt[:, k, cs], mix[:, k, :], k == 0, k == KC - 1, [wtk, mixk[k]], [pok])
                TT(H[:, c, :], po[:, :], H[:, c, :], ALU.add, [pok, Hk[c]], [Hk[c]])
            yield
            if stage == 3:
                yield from layer_norm(H, Hk, GS, C_LN[3][0], C_LN[3][1], "ln", pool, out_f=H, out_fk=Hk)
                DMA("sp", outT[:, :, gsl], H[:], "OUT_%d" % g, Hk, ["out%d" % g])
                return
            yield from layer_norm(H, Hk, GS, C_LN[3][0], C_LN[3][1], "ln", pool, out_f=H, out_fk=Hk,
                                  agcol=C_LN[3][0], abcol=C_LN[3][1], out_b=Hb, out_bk=Hbk)

        def gen_ffn2(g, pool):
            gsl = slice(g * GS, (g + 1) * GS)
            Hb, Hbk = Hb2[g % 2], Hbk2[g % 2]
            yield from ffn(w2g, w2u, w2d, Hb, Hbk, pool, "w2", g == 0)
            yield from layer_norm(H, Hk, GS, C_LN[4][0], C_LN[4][1], "ln", pool, out_f=H, out_fk=Hk)
            CTX("store", pool)
            DMA("sp", outT[:, :, gsl], H[:], "OUT_%d" % g, Hk, ["out%d" % g])
            yield

        interleave(chain(gen_gu1(0, "A"), gen_down1(0, "A")), 32, gen_setup_kv("B"), 15)
        if stage >= 2:
            for g in range(NG):
                A = chain(gen_ln1(g, "B"), gen_hg(g, "B"))
                if g + 1 < NG:
                    interleave(A, 52, chain(gen_gu1(g + 1, "A"), gen_down1(g + 1, "A")), 32)
                else:
                    interleave(A, 52, gen_mixer_sgu(0, "A", (GUalt, GUaltk), delay=26, vl=(vln4_alt, ["scr%d" % (13 + i) for i in range(4)]),
                                                    altw=True, gv=([H[:, c_, :] for c_ in range(4)], Hk[0:4])), 36)
        else:
            run(gen_ln1(0, "all"))
            for g in range(1, NG):
                run(chain(gen_gu1(g, "all"), gen_down1(g, "all"), gen_ln1(g, "all")))

        if stage >= 2:
            CTX("xchg", "all")
            DMA("pool", cin.ap().rearrange("(h d) v -> d h v", d=P), Sst[:], "XC", ["S0", "S1", "S2", "S3"], ["cin"])
            S.dma("pool", lambda e: e.collective_compute("AllGather", ALU.bypass, replica_groups=[[0, 1], [2, 3], [4, 5], [6, 7]],
                                                         ins=[cin.ap().opt()], outs=[cout.ap().opt()]),
                  "CC", reads=["cin"], writes=["cout"], inc=1)
            DMA("pool", Sp[:], cout.ap()[0:4 * P, :].rearrange("(h d) v -> d h v", d=P), "XC", ["cout"], ["Sp"])
            TS(Spb[:], Sp[:], flg[:, 0:1], None, ALU.mult, None, ["Sp", "flg"], ["Spb"])

            if stage >= 4:
                run(gen_mixer_fin(0, "all", (GUalt, GUaltk)))
            else:
                run(gen_mixer(0, "all"))
            for g in range(NG):
                run(gen_rest(g, "all"))
                if stage in (2, 3, 20):
                    if g + 1 < NG:
                        run(gen_mixer(g + 1, "all"))
                    continue
                if g + 1 < NG:
                    interleave(gen_ffn2(g, "A"), 33, gen_mixer(g + 1, "B"), 31)
                else:
                    run(gen_ffn2(g, "all"))

        S.wait_all("sp", ["out%d" % g for g in range(NG)])
        S.wait_all("act", ["out%d" % g for g in range(NG)])
        build_program.last_sched = S
        print("SBUF bytes/partition:", budget[0], "instr:", {e: len(v) for e, v in S.streams.items()})
        with nc.Block() as block:
            S.replay(block)
    return nc


def _blk_kn(w, nb):
    K, N = w.shape
    return np.ascontiguousarray(w.reshape(K // P, P, N // nb, nb).transpose(2, 1, 0, 3))


def prepare_inputs(x, mem, ffn1_w_gate, ffn1_w_up, ffn1_w_down, ln1_g, ln1_b, w_in, hg_lb_logits, hg_norm_g,
                   sg_ln_g, sg_ln_b, sg_w_s, sg_b_s, w_out, ln2_g, ln2_b, mem_ln_g, mem_ln_b, xa_w_q, xa_w_k,
                   xa_w_v, xa_w_o, ln3_g, ln3_b, ffn2_w_gate, ffn2_w_up, ffn2_w_down, ln4_g, ln4_b):
    f = lambda a: np.asarray(a, dtype=np.float32)
    col = lambda v: f(v).reshape(KC, P).T
    cols = np.zeros((P, NCOLS), np.float32)
    for idx, (g_, b_) in ((1, (ln1_g, ln1_b)), (2, (ln2_g, ln2_b)), (3, (ln3_g, ln3_b)), (4, (ln4_g, ln4_b)),
                          ("m", (mem_ln_g, mem_ln_b))):
        cols[:, C_LN[idx][0]:C_LN[idx][0] + 8] = col(g_[0])
        cols[:, C_LN[idx][1]:C_LN[idx][1] + 8] = col(b_[0])
    lg = f(hg_lb_logits)
    cols[:, C_LB0:C_LB0 + 4] = lg[0].T
    cols[:, C_LB1:C_LB1 + 4] = lg[1].T
    cols[:, C_HGN] = f(hg_norm_g)[0]
    sgrow = np.stack([f(sg_ln_g)[0].reshape(512), f(sg_ln_b)[0].reshape(512), f(sg_b_s)[0].reshape(512)])
    sgw = np.ascontiguousarray(f(sg_w_s)[0].transpose(1, 0, 2))
    shared = {
        "cols": cols, "sgrow": np.ascontiguousarray(sgrow), "sgw": sgw,
        "w1g": _blk_kn(f(ffn1_w_gate)[0], 256), "w1u": _blk_kn(f(ffn1_w_up)[0], 256),
        "w1d": _blk_kn(f(ffn1_w_down)[0], P),
        "w2g": _blk_kn(f(ffn2_w_gate)[0], 256), "w2u": _blk_kn(f(ffn2_w_up)[0], 256),
        "w2d": _blk_kn(f(ffn2_w_down)[0], P),
        "win": _blk_kn(f(w_in)[0], 512), "wout": _blk_kn(f(w_out)[0], 512),
        "wq": _blk_kn(f(xa_w_q)[0], 512), "wk": _blk_kn(f(xa_w_k)[0], 512),
        "wv": _blk_kn(f(xa_w_v)[0], 512), "wo": _blk_kn(f(xa_w_o)[0], 512),
    }
    x = f(x)
    mem = f(mem)
    in_maps = []
    for c in range(NCORES):
        b, half = c // 2, c % 2
        xs = x[b, half * T:(half + 1) * T, :]
        m = dict(shared)
        m["xT"] = np.ascontiguousarray(xs.reshape(T, KC, P).transpose(2, 1, 0))
        m["memT"] = np.ascontiguousarray(mem[b].reshape(MEM, KC, P).transpose(2, 1, 0))
        m["flag"] = np.full((P, 1), float(half), np.float32)
        in_maps.append(m)
    return in_maps


def assemble(results):
    out = np.zeros((4, 2 * T, D), np.float32)
    for c in range(NCORES):
        b, half = c // 2, c % 2
        o = np.asarray(results[c]["outT"])
        out[b, half * T:(half + 1) * T, :] = o.transpose(2, 1, 0).reshape(T, D)
    return out


def kernel(**inputs):
    in_maps = prepare_inputs(**inputs)
    nc = build_program(STAGE)
    res = run_bass_kernel_spmd(nc, in_maps, core_ids=list(range(NCORES)))
    return assemble(res.results)
```

```python
import numpy as np
from contextlib import ExitStack
import concourse.bass as bass
import concourse.mybir as mybir
from concourse.bass_utils import run_bass_kernel_spmd

F32 = mybir.dt.float32
BF16 = mybir.dt.bfloat16
AF = mybir.ActivationFunctionType
ALU = mybir.AluOpType

NCORES = 8
P = 128
D = 1024
KC = 8
DFF = 2816
FC = 22
NFB = 11
T = 2048
GS = 512
NG = 4
MEM = 256
ALPHA = 2.0 ** 0.25
EPS = 1e-5
STAGE = 4
USE_WCACHE = False

C_LN = {1: (0, 8), 2: (16, 24), 3: (32, 40), 4: (48, 56), "m": (64, 72)}
C_LB0, C_LB1, C_HGN, NCOLS = 80, 84, 88, 89

ENGS = ("pe", "act", "dve", "pool", "sp")


class Sched:
    def __init__(self, nc, stack):
        self.nc = nc
        self.stack = stack
        self.streams = {e: [] for e in ENGS}
        self.count = {}
        self.sems = {}
        self.waited = {e: {} for e in ENGS}
        self.lastw = {}
        self.readers = {}
        self.same_engine_sync = True
        self.tag = ""
        self.alias = {}
        for e in ENGS:
            self._sem("E_" + e)

    def _sem(self, key):
        if key not in self.sems:
            self.sems[key] = self.stack.enter_context(self.nc.semaphore(key))
            self.count[key] = 0
        return self.sems[key]

    def _deps(self, eng, reads, writes):
        deps = {}

        def add(d):
            if d is not None and deps.get(d[0], 0) < d[1]:
                deps[d[0]] = d[1]

        for k in reads:
            add(self.lastw.get(k))
        for k in writes:
            add(self.lastw.get(k))
            for rk, rv in self.readers.get(k, {}).items():
                add((rk, rv))
        for k, v in deps.items():
            if k == "E_" + eng and (eng == "pe" or not self.same_engine_sync):
                continue
            if self.waited[eng].get(k, 0) >= v:
                continue
            self.waited[eng][k] = v
            self.streams[eng].append(("wait", k, v))

    def _mark(self, semkey, val, reads, writes):
        for k in writes:
            self.lastw[k] = (semkey, val)
            self.readers[k] = {}
        for k in reads:
            if k in writes:
                continue
            self.readers.setdefault(k, {})[semkey] = val

    def op(self, eng, fn, reads=(), writes=()):
        reads = tuple(self.alias.get(k, k) for k in reads)
        writes = tuple(self.alias.get(k, k) for k in writes)
        self._deps(eng, reads, writes)
        semkey = "E_" + eng
        self.count[semkey] += 1
        self.streams[eng].append(("op", fn, semkey, 1, self.tag))
        self._mark(semkey, self.count[semkey], reads, writes)

    def dma(self, eng, fn, semkey, reads=(), writes=(), inc=16):
        reads = tuple(self.alias.get(k, k) for k in reads)
        writes = tuple(self.alias.get(k, k) for k in writes)
        self._sem(semkey)
        self._deps(eng, reads, writes)
        self.count[semkey] += inc
        self.streams[eng].append(("op", fn, semkey, inc, self.tag))
        self._mark(semkey, self.count[semkey], reads, writes)

    def fence(self, keys, semkey):
        for k in keys:
            self.lastw[self.alias.get(k, k)] = (semkey, self.count[semkey])

    def wait_all(self, eng, keys):
        self._deps(eng, tuple(self.alias.get(k, k) for k in keys), ())

    def barrier(self):
        for e in ENGS:
            for k, v in self.count.items():
                if v > 0 and k != "E_" + e and self.waited[e].get(k, 0) < v:
                    self.waited[e][k] = v
                    self.streams[e].append(("wait", k, v))

    def replay(self, block):
        sched = self

        def run(engname):
            def body(e):
                for item in sched.streams[engname]:
                    if item[0] == "wait":
                        e.wait_ge(sched.sems[item[1]], item[2])
                    else:
                        _, fn, semkey, inc, _tag = item
                        fn(e).then_inc(sched.sems[semkey], inc)
            return body

        for name, sec in (("sp", block.sync), ("pe", block.tensor), ("act", block.scalar),
                          ("dve", block.vector), ("pool", block.gpsimd)):
            if self.streams[name]:
                sec(run(name))


def build_program(stage=STAGE):
    nc = bass.Bass("TRN2", target_bir_lowering=False)

    def din(name, shape):
        return nc.dram_tensor(name, list(shape), F32, kind="ExternalInput").ap()

    xT = din("xT", [P, KC, T])
    memT = din("memT", [P, KC, MEM])
    flag = din("flag", [P, 1])
    cols = din("cols", [P, NCOLS])
    sgrow = din("sgrow", [3, 512])
    sgw = din("sgw", [P, 4, P])
    w1g = din("w1g", [NFB, P, KC, 256])
    w1u = din("w1u", [NFB, P, KC, 256])
    w1d = din("w1d", [KC, P, FC, P])
    w2g = din("w2g", [NFB, P, KC, 256])
    w2u = din("w2u", [NFB, P, KC, 256])
    w2d = din("w2d", [KC, P, FC, P])
    win = din("win", [6, P, KC, 512])
    wout = din("wout", [2, P, KC, 512])
    wq = din("wq", [2, P, KC, 512])
    wk = din("wk", [2, P, KC, 512])
    wv = din("wv", [2, P, KC, 512])
    wo = din("wo", [2, P, KC, 512])
    outT = nc.dram_tensor("outT", [P, KC, T], F32, kind="ExternalOutput").ap()
    hsp = nc.dram_tensor("hsp", [NG, P, KC * GS], F32)
    hsb = nc.dram_tensor("hsb", [NG, P, KC * GS], BF16)
    osp = nc.dram_tensor("osp", [NG, P, 4 * GS], F32)
    qsp = nc.dram_tensor("qsp", [NG, P, 4 * GS], BF16)
    wcache = {nm: nc.dram_tensor("c_" + nm, shp, BF16) for nm, shp in (
        ("w1g", [NFB, P, KC * 256]), ("w1u", [NFB, P, KC * 256]), ("w1d", [KC, P, FC * P]),
        ("w2g", [NFB, P, KC * 256]), ("w2u", [NFB, P, KC * 256]), ("w2d", [KC, P, FC * P]))}
    cin = nc.dram_tensor("cin", [4 * P, P], F32)
    cout = nc.dram_tensor("cout", [8 * P, P], F32)

    with ExitStack() as st:
        S = Sched(nc, st)
        budget = [0]

        def sb(name, shape, dt):
            n = 1
            for s_ in shape[1:]:
                n *= s_
            budget[0] += n * (4 if dt == F32 else 2)
            return st.enter_context(nc.sbuf_tensor(name, list(shape), dt))

        H = sb("H", [P, KC, GS], F32)
        Hb2 = [sb("Hb_%d" % i, [P, KC, GS], BF16) for i in range(2)]
        OL = sb("OL", [P, 4, GS], F32)
        QP = sb("QP", [P, 4, GS], BF16)
        SCRB = sb("SCRB", [P, FC * GS], BF16)
        HGB = sb("HGB", [P, 19 * GS], BF16)
        TFR = sb("TFR", [P, 17 * GS], F32)
        mix = sb("mix", [P, KC, GS], BF16)
        WG = [sb("WG%d" % i, [P, KC, 256], BF16) for i in range(2)]
        WU = [sb("WU%d" % i, [P, KC, 256], BF16) for i in range(2)]
        WD = [sb("WD%d" % i, [P, FC, P], BF16) for i in range(2)]
        WB = [sb("WB%d" % i, [P, KC, 512], BF16) for i in range(2)]
        KT = sb("KT", [P, KC, MEM], BF16)
        Vm = sb("Vm", [P, 2, D], BF16)
        ident_f = sb("ident_f", [P, P], F32)
        ident_b = sb("ident_b", [P, P], BF16)
        onesK = sb("onesK", [P, P], F32)
        ones128 = sb("ones128", [P, P], F32)
        ones_b = sb("ones_b", [P, P], BF16)
        onesKb = sb("onesKb", [P, P], BF16)
        sqb = [sb("sqb%d" % i, [P, GS], BF16) for i in range(2)]
        MASK4 = sb("MASK4", [P, 4, P], BF16)
        rmask = sb("rmask", [P, GS], F32)
        WsT = sb("WsT", [P, 4, P], BF16)
        SGG = sb("SGG", [P, 512], F32)
        SGB = sb("SGB", [P, 512], F32)
        bs_hi = sb("bs_hi", [1, 512], BF16)
        bs_lo = sb("bs_lo", [1, 512], BF16)
        bs_lo2 = sb("bs_lo2", [1, 512], BF16)
        colt = sb("colt", [P, NCOLS], F32)
        colA = sb("colA", [P, 64], F32)
        lbc = sb("lbc", [P, 4], F32)
        omlc = sb("omlc", [P, 4], F32)
        flg = sb("flg", [P, 1], F32)
        Sst = sb("Sst", [P, 4, P], F32)
        Sbf = [[sb("Sbf%d_%d" % (h, i), [P, P], BF16) for i in range(2)] for h in range(4)]
        Sp = sb("Sp", [P, 4, P], F32)
        Spb = sb("Spb", [P, 4, P], BF16)
        LPX = sb("LPX", [P, 4, 9], F32)
        PXt = sb("PXt", [P, 4, 8], F32)
        EPS_AP = sb("epsc", [P, 1], F32)

        def tf(i, n=1):
            return TFR[:, i * GS:(i + n) * GS]
        tA, tB, tC, tD, tE, tF = (tf(i) for i in range(6))
        sl = [tf(6), tf(7)]
        E2 = [tf(8 + h) for h in range(4)]
        GU = TFR[:, 8 * GS:12 * GS].rearrange("p (g t) -> p g t", t=GS)
        lnT = [tf(12 + i) for i in range(5)]
        memf = TFR[:, 8 * GS:12 * GS].rearrange("p (c m) -> p c m", m=MEM)
        sgwt = TFR[:, 0:512].rearrange("p (g s) -> p g s", s=P)
        bsr = TFR[0:1, 2 * GS:3 * GS]
        bs_hf = TFR[0:1, 3 * GS:4 * GS]
        hmid = SCRB[:, :].rearrange("p (j t) -> p j t", t=GS)

        def bv(i, n=1):
            return SCRB[:, i * GS:(i + n) * GS]

        def hv(i, n=1):
            return HGB[:, i * GS:(i + n) * GS]
        qb = [hv(h) for h in range(4)]
        kdT = [hv(4 + h) for h in range(4)]
        scTs = [hv(8 + h).rearrange("p (i t) -> p i t", t=P) for h in range(4)]
        vt = hv(12, 4).rearrange("p (i f) -> p i f", f=512)
        qc, kc, kdec = hv(16), hv(17), hv(18)
        QT = bv(1, 8).rearrange("p (c t) -> p c t", t=GS)
        PT = [bv(9), bv(10)]
        memb = hv(0, 4).rearrange("p (c m) -> p c m", m=MEM)

        AL = S.alias
        for i_, n_ in enumerate(("tA", "tB", "tC", "tD", "tE", "tF", "sl0", "sl1")):
            AL[n_] = "tf%d" % i_
        for h_ in range(4):
            AL["E2_%d" % h_] = "tf%d" % (8 + h_)
            AL["GU%d" % h_] = "tf%d" % (8 + h_)
        for c_ in range(KC):
            AL["QT%d" % c_] = "scr%d" % (1 + c_)
        AL["PT0"], AL["PT1"] = "scr9", "scr10"
        AL["sgwt"], AL["bsr"], AL["bs_hf"] = "tf0", "tf2", "tf3"
        for j_ in range(FC):
            AL["hm%d" % j_] = "scr%d" % j_

        PS = [st.enter_context(nc.psum_tensor("PS%d" % i, [P, 512], F32)) for i in range(8)]
        pools = {"all": [0, 1, 2, 3, 4, 5], "A": [0, 1, 2, 3], "B": [4, 5]}
        rr = {"all": 0, "A": 0, "B": 0}
        ctx_ = {"pool": "all"}

        def bank():
            p_ = ctx_["pool"]
            i = pools[p_][rr[p_] % len(pools[p_])]
            rr[p_] += 1
            return PS[i], "PS%d" % i

        def CTX(tag, pool):
            S.tag = tag
            ctx_["pool"] = pool

        def MM(out, lhsT, rhs, start, stop, reads, writes):
            S.op("pe", lambda e: e.matmul(out, lhsT=lhsT, rhs=rhs, start=start, stop=stop),
                 reads=reads, writes=writes)

        def ACT(out, in_, func, reads, writes, scale=None, bias=None):
            kw = {}
            if scale is not None:
                kw["scale"] = scale
            if bias is not None:
                kw["bias"] = bias
            S.op("act", lambda e: e.activation(out=out, in_=in_, func=func, **kw), reads=reads, writes=writes)

        def TT(out, in0, in1, op, reads, writes, eng="dve"):
            S.op(eng, lambda e: e.tensor_tensor(out=out, in0=in0, in1=in1, op=op), reads=reads, writes=writes)

        def TS(out, in0, s1, s2, op0, op1, reads, writes, eng="dve"):
            if op1 is None:
                S.op(eng, lambda e: e.tensor_scalar(out=out, in0=in0, scalar1=s1, scalar2=None, op0=op0),
                     reads=reads, writes=writes)
            else:
                S.op(eng, lambda e: e.tensor_scalar(out=out, in0=in0, scalar1=s1, scalar2=s2, op0=op0, op1=op1),
                     reads=reads, writes=writes)

        def STT(out, in0, scalar, in1, op0, op1, reads, writes):
            S.op("dve", lambda e: e.scalar_tensor_tensor(out=out, in0=in0, scalar=scalar, in1=in1, op0=op0, op1=op1),
                 reads=reads, writes=writes)

        def DMA(eng, out, in_, sem, reads, writes):
            S.dma(eng, lambda e: e.dma_start(out=out, in_=in_), sem, reads=reads, writes=writes)

        def run(gen):
            for _ in gen:
                pass

        def interleave(ga, na, gb, nb):
            ia = ib = 0
            da = db = False
            while not (da and db):
                if not da and (db or ia * nb <= ib * na):
                    try:
                        next(ga)
                        ia += 1
                    except StopIteration:
                        da = True
                else:
                    try:
                        next(gb)
                        ib += 1
                    except StopIteration:
                        db = True

        Hk = ["H%d" % c for c in range(KC)]
        Hbk2 = [["Hb%d_%d" % (i, c) for c in range(KC)] for i in range(2)]
        mixk = ["mix%d" % c for c in range(KC)]

        setup_keys = ["colt", "SGG", "SGB", "bsr", "flg", "sgwt", "E2_0", "E2_1", "E2_2", "E2_3"]
        DMA("sp", colt[:], cols, "SETUP", [], ["colt"])
        DMA("sp", SGG[:], sgrow[0, :].partition_broadcast(P), "SETUP", [], ["SGG"])
        DMA("sp", SGB[:], sgrow[1, :].partition_broadcast(P), "SETUP", [], ["SGB"])
        DMA("sp", bsr, sgrow[2:3, :], "SETUP", [], ["bsr"])
        DMA("sp", flg[:], flag, "SETUP", [], ["flg"])
        DMA("sp", sgwt, sgw, "SETUP", [], ["sgwt"])
        DMA("sp", memf, memT, "SETUP", [], ["E2_0", "E2_1", "E2_2", "E2_3"])
        S.fence(setup_keys, "SETUP")

        S.op("pool", lambda e: e.memset(EPS_AP[:], EPS), writes=["epsc"])
        S.op("pool", lambda e: e.memset(ident_f[:], 0.0), writes=["ident_f"])
        S.op("pool", lambda e: e.affine_select(out=ident_f[:], in_=ident_f[:], pattern=[[-1, P]],
                                               compare_op=ALU.not_equal, fill=1.0, base=0, channel_multiplier=1),
             reads=["ident_f"], writes=["ident_f"])
        S.op("dve", lambda e: e.tensor_copy(out=ident_b[:], in_=ident_f[:]), reads=["ident_f"], writes=["ident_b"])
        S.op("pool", lambda e: e.memset(onesK[:], 1.0 / D), writes=["onesK"])
        S.op("pool", lambda e: e.memset(ones128[:], 1.0 / P), writes=["ones128"])
        S.op("pool", lambda e: e.memset(ones_b[:], 1.0), writes=["ones_b"])
        S.op("pool", lambda e: e.memset(onesKb[:], 1.0 / D), writes=["onesKb"])
        S.op("pool", lambda e: e.memset(rmask[:], 1.0), writes=["rmask"])
        S.op("pool", lambda e: e.memset(rmask[:].rearrange("p (c t) -> p c t", t=64)[:, :, 0:1], 0.0),
             reads=["rmask"], writes=["rmask"])
        S.op("pool", lambda e: e.memset(tB[:, 0:P], 1.0), writes=["tB"])
        S.op("pool", lambda e: e.affine_select(out=tB[:, 0:P], in_=tB[:, 0:P], pattern=[[1, P]],
                                               compare_op=ALU.is_ge, fill=0.0, base=0, channel_multiplier=-1),
             reads=["tB"], writes=["tB"])
        S.op("pool", lambda e: e.memset(tB[0:64, 64:P], 0.0), reads=["tB"], writes=["tB"])
        for i in range(4):
            S.op("dve", (lambda i: lambda e: e.tensor_copy(out=MASK4[:, i, :], in_=tB[:, 0:P]))(i),
                 reads=["tB"], writes=["MASK4"])
        S.op("pool", lambda e: e.memset(Sst[:], 0.0), writes=["S0", "S1", "S2", "S3"])
        for h in range(4):
            S.op("pool", (lambda h: lambda e: e.memset(Sbf[h][0][:], 0.0))(h), writes=["Sbf%d_0" % h])
        S.op("pool", lambda e: e.memset(LPX[:], 0.0), writes=["LPX0", "LPX1", "LPX2", "LPX3"])
        TS(colA[:], colt[:, 0:64], ALPHA, None, ALU.mult, None, ["colt"], ["colA"])
        TT(lbc[:], colt[:, C_LB0:C_LB0 + 4], colt[:, C_LB1:C_LB1 + 4], ALU.subtract, ["colt"], ["lbc"])
        ACT(lbc[:], lbc[:], AF.Sigmoid, ["lbc"], ["lbc"])
        TS(omlc[:], lbc[:], -1.0, 1.0, ALU.mult, ALU.add, ["lbc"], ["omlc"])
        S.op("dve", lambda e: e.tensor_copy(out=bs_hi[:], in_=bsr), reads=["bsr"], writes=["bs_hi"])
        S.op("dve", lambda e: e.tensor_copy(out=bs_hf, in_=bs_hi[:]), reads=["bs_hi"], writes=["bs_hf"])
        TT(bsr, bsr, bs_hf, ALU.subtract, ["bsr", "bs_hf"], ["bsr"])
        S.op("dve", lambda e: e.tensor_copy(out=bs_lo[:], in_=bsr), reads=["bsr"], writes=["bs_lo"])
        S.op("dve", lambda e: e.tensor_copy(out=bs_hf, in_=bs_lo[:]), reads=["bs_lo"], writes=["bs_hf"])
        TT(bs_lo2[:], bsr, bs_hf, ALU.subtract, ["bsr", "bs_hf"], ["bs_lo2"])
        for g_ in range(4):
            S.op("pool", (lambda g_: lambda e: e.affine_select(out=sgwt[:, g_, :], in_=sgwt[:, g_, :], pattern=[[-1, P]],
                                                               compare_op=ALU.is_ge, fill=0.0, base=0, channel_multiplier=1))(g_),
                 reads=["sgwt"], writes=["sgwt"])
        pb, pk = bank()
        for g_ in range(4):
            S.op("pe", (lambda g_: lambda e: e.transpose(pb[:, g_ * P:(g_ + 1) * P], in_=sgwt[:, g_, :], identity=ident_f[:]))(g_),
                 reads=["sgwt", "ident_f"], writes=[pk])
        S.op("dve", lambda e: e.tensor_copy(out=WsT[:], in_=pb[:, :].rearrange("p (g t) -> p g t", t=P)),
             reads=[pk], writes=["WsT"])

        def layer_norm(src, srck, N, gcol, bcol, tag, pool, out_f=None, out_fk=None, agcol=None, abcol=None,
                       out_b=None, out_bk=None):
            CTX(tag, pool)
            p1, k1 = bank()
            for c in range(KC):
                MM(p1[:, 0:N], onesK[:], src[:, c, :], c == 0, c == KC - 1, [srck[c], "onesK"], [k1])
            p2, k2 = bank()
            for c in range(KC):
                sq, sqk = sqb[c % 2][:, 0:N], "sqb%d" % (c % 2)
                ACT(sq, src[:, c, :], AF.Square, [srck[c]], [sqk])
                MM(p2[:, 0:N], onesKb[:], sq, c == 0, c == KC - 1, [sqk, "onesKb"], [k2])
            yield
            CTX(tag, pool)
            mean, rs_, nm_ = lnT[0][:, 0:N], lnT[1][:, 0:N], lnT[2][:, 0:N]
            ACT(mean, p1[:, 0:N], AF.Copy, [k1], ["lnA"])
            TT(rs_, mean, mean, ALU.mult, ["lnA"], ["lnB"])
            TT(rs_, p2[:, 0:N], rs_, ALU.subtract, [k2, "lnB"], ["lnB"])
            ACT(rs_, rs_, AF.Ln, ["lnB", "epsc"], ["lnB"], bias=EPS_AP[:, 0:1])
            ACT(rs_, rs_, AF.Exp, ["lnB"], ["lnB"], scale=-0.5)
            STT(nm_, mean, -1.0, rs_, ALU.mult, ALU.mult, ["lnA", "lnB"], ["lnC"])
            for c in range(KC):
                tmp, tk = (lnT[3], "lnD") if c % 2 == 0 else (lnT[4], "lnE")
                tmp = tmp[:, 0:N]
                TT(tmp, src[:, c, :], rs_, ALU.mult, [srck[c], "lnB"], [tk])
                TT(tmp, tmp, nm_, ALU.add, [tk, "lnC"], [tk])
                if out_b is not None:
                    ACT(out_b[:, c, :], tmp, AF.Identity, [tk, "colt"], [out_bk[c]],
                        scale=colt[:, gcol + c:gcol + c + 1], bias=colt[:, bcol + c:bcol + c + 1])
                if out_f is not None:
                    if agcol is not None:
                        sc_, bi_ = colA[:, agcol + c:agcol + c + 1], colA[:, abcol + c:abcol + c + 1]
                    else:
                        sc_, bi_ = colt[:, gcol + c:gcol + c + 1], colt[:, bcol + c:bcol + c + 1]
                    ACT(out_f[:, c, :], tmp, AF.Identity, [tk, "colt", "colA"], [out_fk[c]], scale=sc_, bias=bi_)
                if c % 4 == 3:
                    yield
                    CTX(tag, pool)

        memk = ["E2_%d" % (c // 2) for c in range(KC)]
        membk = ["qb%d" % (c // 2) for c in range(KC)]
        wload_ctr = [0]

        def load_wb(src_ap):
            i = wload_ctr[0] % 2
            wload_ctr[0] += 1
            DMA("pool", WB[i][:], src_ap, "WB%d" % i, [], ["WB%d" % i])
            return WB[i], "WB%d" % i

        def gen_setup_kv(pool):
            yield from layer_norm(memf, memk, MEM, C_LN["m"][0], C_LN["m"][1], "setup", pool, out_b=memb, out_bk=membk)
            for blk in range(2):
                CTX("setup", pool)
                wt, wkk = load_wb(wk[blk])
                for cc in range(4):
                    CTX("setup", pool)
                    fc_ = blk * 4 + cc
                    pb, pk = bank()
                    for k in range(KC):
                        MM(pb[:, 0:MEM], wt[:, k, cc * P:(cc + 1) * P], memb[:, k, :], k == 0, k == KC - 1, [wkk, membk[k]], [pk])
                    ACT(KT[:, fc_, :], pb[:, 0:MEM], AF.Copy, [pk], ["KT"])
                    yield
            for blk in range(2):
                CTX("setup", pool)
                wt, wkk = load_wb(wv[blk])
                for mt in range(2):
                    CTX("setup", pool)
                    pb, pk = bank()
                    for k in range(KC):
                        MM(pb[:, :], memb[:, k, mt * P:(mt + 1) * P], wt[:, k, :], k == 0, k == KC - 1, [wkk, membk[k]], [pk])
                    ACT(Vm[:, mt, blk * 512:(blk + 1) * 512], pb[:, :], AF.Copy, [pk], ["Vm"])
                    yield

        gu_ctr = [0]
        wd_ctr = [0]

        def wload(buf, bufk, src_d, nm, idx, first, pat, n_):
            ck = "wc_%s_%d" % (nm, idx)
            cview = wcache[nm][idx].rearrange(pat, **{pat.split("(")[1].split(")")[0].split()[1]: n_})
            if not USE_WCACHE:
                DMA("pool", buf[:], src_d[idx], bufk, [], [bufk])
            elif first:
                DMA("pool", buf[:], src_d[idx], bufk, [], [bufk])
                DMA("sp", cview, buf[:], "CW_" + bufk, [bufk], [ck])
            else:
                DMA("sp", buf[:], cview, bufk, [ck], [bufk])

        def ffn(wg_d, wu_d, wd_d, Hb, Hbk, pool, nm=None, first=True):
            yield from ffn_gu(wg_d, wu_d, Hb, Hbk, pool, nm, first)
            yield from ffn_down(wd_d, pool, nm, first)

        def ffn_gu(wg_d, wu_d, Hb, Hbk, pool, nm=None, first=True):
            for jb in range(NFB):
                CTX("ffn_gu", pool)
                b_ = gu_ctr[0] % 2
                gu_ctr[0] += 1
                wload(WG[b_], "WG%d" % b_, wg_d, nm + "g", jb, first, "p (c n) -> p c n", 256)
                wload(WU[b_], "WU%d" % b_, wu_d, nm + "u", jb, first, "p (c n) -> p c n", 256)
                for jj in range(2):
                    j = jb * 2 + jj
                    pg, pgk = bank()
                    for k in range(KC):
                        MM(pg[:, :], WG[b_][:, k, jj * P:(jj + 1) * P], Hb[:, k, :], k == 0, k == KC - 1,
                           ["WG%d" % b_, Hbk[k]], [pgk])
                    pu, puk = bank()
                    for k in range(KC):
                        MM(pu[:, :], WU[b_][:, k, jj * P:(jj + 1) * P], Hb[:, k, :], k == 0, k == KC - 1,
                           ["WU%d" % b_, Hbk[k]], [puk])
                    s_, sk_ = sl[j % 2], "sl%d" % (j % 2)
                    ACT(s_, pg[:, :], AF.Silu, [pgk], [sk_])
                    TT(hmid[:, j, :], s_, pu[:, :], ALU.mult, [sk_, puk], ["hm%d" % j])
                    yield
                    CTX("ffn_gu", pool)

        def ffn_down(wd_d, pool, nm=None, first=True):
            for c in range(KC):
                CTX("ffn_down", pool)
                b_ = wd_ctr[0] % 2
                wd_ctr[0] += 1
                wload(WD[b_], "WD%d" % b_, wd_d, nm + "d", c, first, "p (j n) -> p j n", P)
                po, pok = bank()
                for j in range(FC):
                    MM(po[:, :], WD[b_][:, j, :], hmid[:, j, :], j == 0, j == FC - 1, ["WD%d" % b_, "hm%d" % j], [pok])
                STT(H[:, c, :], po[:, :], 0.5, H[:, c, :], ALU.mult, ALU.add, [pok, Hk[c]], [Hk[c]])
                yield

        ln1_done = [False] * NG

        def gen_gu1(g, pool):
            gsl = slice(g * GS, (g + 1) * GS)
            Hb, Hbk = Hb2[g % 2], Hbk2[g % 2]
            CTX("xload", pool)
            DMA("pool", Hb[:], xT[:, :, gsl], "LXB_%d" % g, [], Hbk)
            yield
            yield from ffn_gu(w1g, w1u, Hb, Hbk, pool, "w1", g == 0)

        def gen_down1(g, pool):
            gsl = slice(g * GS, (g + 1) * GS)
            assert g == 0 or ln1_done[g - 1], "H still owned by the previous group's LayerNorm"
            CTX("xload", pool)
            DMA("sp", H[:], xT[:, :, gsl], "LHx_%d" % g, [], Hk)
            ACT(H[:], H[:], AF.Identity, Hk, Hk, scale=ALPHA)
            yield
            yield from ffn_down(w1d, pool, "w1", g == 0)

        def gen_ln1(g, pool):
            gsl = slice(g * GS, (g + 1) * GS)
            Hb, Hbk = Hb2[g % 2], Hbk2[g % 2]
            if stage == 1:
                yield from layer_norm(H, Hk, GS, C_LN[1][0], C_LN[1][1], "ln", pool, out_f=H, out_fk=Hk)
                DMA("sp", outT[:, :, gsl], H[:], "OUT_%d" % g, Hk, ["out%d" % g])
                ln1_done[g] = True
                return
            yield from layer_norm(H, Hk, GS, C_LN[1][0], C_LN[1][1], "ln", pool, out_f=H, out_fk=Hk,
                                  agcol=C_LN[1][0], abcol=C_LN[1][1], out_b=Hb, out_bk=Hbk)
            CTX("spill", pool)
            DMA("sp", hsp[g].rearrange("p (c t) -> p c t", t=GS), H[:], "SPL_%d" % g, Hk, ["hsp%d" % g])
            DMA("sp", hsb[g].rearrange("p (c t) -> p c t", t=GS), Hb[:], "SPL2_%d" % g, Hbk, ["hsb%d" % g])
            ln1_done[g] = True
            yield

        def chain(*gens):
            for g_ in gens:
                yield from g_

        cur = [0, 0, 0, 0]

        def gen_hg(g, pool):
            Hb, Hbk = Hb2[g % 2], Hbk2[g % 2]
            CTX("hg_prep", pool)
            wi_, wik = load_wb(win[2])
            for i in range(4):
                CTX("hg_prep", pool)
                pb, pk = bank()
                for k in range(KC):
                    MM(pb[:, :], Hb[:, k, i * P:(i + 1) * P], wi_[:, k, :], k == 0, k == KC - 1, [Hbk[k], wik], [pk])
                ACT(vt[:, i, :], pb[:, :], AF.Copy, [pk], ["vt%d" % i])
                yield
            CTX("hg_prep", pool)
            wq_, wqk = load_wb(win[0])
            wf_, wfk = load_wb(win[1])
            for h in range(4):
                CTX("hg_prep", pool)
                hs = slice(h * P, (h + 1) * P)
                pq, pqk = bank()
                for k in range(KC):
                    MM(pq[:, :], wq_[:, k, hs], Hb[:, k, :], k == 0, k == KC - 1, [wqk, Hbk[k]], [pqk])
                pf, pfk = bank()
                for k in range(KC):
                    MM(pf[:, :], wf_[:, k, hs], Hb[:, k, :], k == 0, k == KC - 1, [wfk, Hbk[k]], [pfk])
                ACT(tA, pf[:, :], AF.Sigmoid, [pfk], ["tA"])
                TS(tA, tA, omlc[:, h:h + 1], lbc[:, h:h + 1], ALU.mult, ALU.add, ["tA", "omlc", "lbc"], ["tA"])
                yield
                CTX("hg_prep", pool)
                ACT(tB, tA, AF.Ln, ["tA"], ["tB"])
                TS(tA, tA, -1.0, 1.0, ALU.mult, ALU.add, ["tA"], ["tA"])
                S.op("dve", lambda e: e.tensor_tensor_scan(out=tC, data0=rmask[:], data1=tB, initial=0.0,
                                                           op0=ALU.mult, op1=ALU.add),
                     reads=["rmask", "tB"], writes=["tC"])
                yield
                CTX("hg_prep", pool)
                b3 = tC.rearrange("p (c t) -> p c t", t=64)
                TT(tD.rearrange("p (c t) -> p c t", t=64), b3, b3[:, :, 31:32].to_broadcast([P, 8, 64]), ALU.subtract,
                   ["tC"], ["tD"])
                TT(tE.rearrange("p (c t) -> p c t", t=64), b3[:, :, 63:64].to_broadcast([P, 8, 64]), b3, ALU.subtract,
                   ["tC"], ["tE"])
                ACT(tB, tD, AF.Exp, ["tD"], ["tB"])
                TT(qc, pq[:, :], tB, ALU.mult, [pqk, "tB"], ["qc"])
                yield
                CTX("hg_prep", pool)
                ACT(tF, tD, AF.Exp, ["tD"], ["tF"], scale=-1.0)
                TT(kc, tA, tF, ALU.mult, ["tA", "tF"], ["kc"])
                yield
                CTX("hg_prep", pool)
                ACT(E2[h], tC, AF.Exp, ["tC"], ["E2_%d" % h])
                TT(qb[h], pq[:, :], E2[h], ALU.mult, [pqk, "E2_%d" % h], ["qb%d" % h])
                yield
                CTX("hg_prep", pool)
                ACT(tE, tE, AF.Exp, ["tE"], ["tE"])
                TT(kdec, tA, tE, ALU.mult, ["tA", "tE"], ["kdec"])
                yield
                CTX("hg_prep", pool)
                S.op("dve", (lambda h: lambda e: e.tensor_tensor_scan(
                    out=LPX[:, h, 1:9], data0=rmask[:, 1:9], data1=tC.rearrange("p (c t) -> p c t", t=64)[:, :, 63],
                    initial=LPX[:, h, 0:1], op0=ALU.mult, op1=ALU.add))(h),
                    reads=["rmask", "tC", "LPX%d" % h], writes=["LPX%d" % h])
                ACT(PXt[:, h, :], LPX[:, h, 0:8], AF.Exp, ["LPX%d" % h], ["PXt%d" % h])
                TT(QP[:, h, :].rearrange("p (c t) -> p c t", t=64), qb[h].rearrange("p (c t) -> p c t", t=64),
                   PXt[:, h, :].unsqueeze(2).to_broadcast([P, 8, 64]), ALU.mult,
                   ["qb%d" % h, "PXt%d" % h], ["QP%d" % h])
                S.op("dve", (lambda h: lambda e: e.tensor_copy(out=LPX[:, h, 0:1], in_=LPX[:, h, 8:9]))(h),
                     reads=["LPX%d" % h], writes=["LPX%d" % h])
                yield
                CTX("hg_prep", pool)
                pt, ptk = bank()
                ptb = pt[:, :].bitcast(BF16)
                for i in range(4):
                    S.op("pe", (lambda i, ptb: lambda e: e.transpose(ptb[:, i * P:(i + 1) * P], in_=kdec[:, i * P:(i + 1) * P],
                                                                      identity=ident_b[:]))(i, ptb),
                         reads=["kdec", "ident_b"], writes=[ptk])
                ACT(kdT[h], ptb[:, 0:GS], AF.Copy, [ptk], ["kdT%d" % h])
                yield
                CTX("hg_prep", pool)
                psc, psck = bank()
                for i in range(4):
                    ts_ = slice(i * P, (i + 1) * P)
                    MM(psc[:, ts_], kc[:, ts_], qc[:, ts_], True, True, ["kc", "qc"], [psck])
                TT(scTs[h], psc[:, :].rearrange("p (i t) -> p i t", t=P), MASK4[:], ALU.mult, [psck, "MASK4"], ["scTs%d" % h])
                yield
            for i in range(4):
                CTX("hg_scan", pool)
                po, pok = bank()
                for half in range(2):
                    c = 2 * i + half
                    gc = g * 8 + c
                    rows = slice(half * 64, half * 64 + 64)
                    cols_ = slice(i * P + half * 64, i * P + half * 64 + 64)
                    for h in range(4):
                        hs = slice(h * P, (h + 1) * P)
                        slot = PS[6 + gc % 2][:, hs]
                        MM(slot, kdT[h][rows, i * P:(i + 1) * P], vt[rows, i, hs], True, True,
                           ["kdT%d" % h, "vt%d" % i], ["PSd%d" % (gc % 2)])
                    for h in range(4):
                        hs = slice(h * P, (h + 1) * P)
                        ocol = slice(h * P + half * 64, h * P + half * 64 + 64)
                        MM(po[:, ocol], vt[rows, i, hs], scTs[h][rows, i, half * 64:half * 64 + 64], True, False,
                           ["vt%d" % i, "scTs%d" % h], [pok])
                        MM(po[:, ocol], Sbf[h][cur[h]][:], qb[h][:, cols_], False, True,
                           ["Sbf%d_%d" % (h, cur[h]), "qb%d" % h], [pok])
                    for h in range(4):
                        slot = PS[6 + gc % 2][:, h * P:(h + 1) * P]
                        STT(Sst[:, h, :], Sst[:, h, :], E2[h][:, cols_.stop - 1:cols_.stop], slot, ALU.mult, ALU.add,
                            ["S%d" % h, "E2_%d" % h, "PSd%d" % (gc % 2)], ["S%d" % h])
                        ACT(Sbf[h][1 - cur[h]][:], Sst[:, h, :], AF.Copy, ["S%d" % h], ["Sbf%d_%d" % (h, 1 - cur[h])])
                        cur[h] = 1 - cur[h]
                ACT(OL[:, :, i * P:(i + 1) * P], po[:, :].rearrange("p (h t) -> p h t", t=P), AF.Copy,
                    [pok], ["OL%d" % i])
                yield
            CTX("spill", pool)
            DMA("sp", osp[g].rearrange("p (h t) -> p h t", t=GS), OL[:], "SPL3_%d" % g, ["OL%d" % i for i in range(4)], ["osp%d" % g])
            DMA("sp", qsp[g].rearrange("p (h t) -> p h t", t=GS), QP[:], "SPL4_%d" % g, ["QP%d" % h for h in range(4)], ["qsp%d" % g])
            yield

        vln4_std = hv(0, 4).rearrange("p (i f) -> p i f", f=512)
        vln4_alt = bv(13, 4).rearrange("p (i f) -> p i f", f=512)
        bst16 = sb("bst16", [P, 16, 6], F32)
        bmv16 = sb("bmv16", [P, 16, 2], F32)
        r16 = sb("r16", [P, 16], F32)
        n16 = sb("n16", [P, 16], F32)
        gV = gV_std = [tA, tB, tC, tD]
        gVk = gVk_std = ["tA", "tB", "tC", "tD"]
        sqs = [(tE, "tE"), (tF, "tF"), (sl[0], "sl0"), (sl[1], "sl1")]

        GUalt = TFR[:, 12 * GS:16 * GS].rearrange("p (g t) -> p g t", t=GS)
        GUaltk = ["lnA", "lnB", "lnC", "lnD"]
        GUstd = (GU, ["GU%d" % gg for gg in range(4)])

        def gen_mixer(g, pool, gu=None, delay=0):
            yield from gen_mixer_sgu(g, pool, gu, delay)
            yield from gen_mixer_fin(g, pool, gu)

        def gen_mixer_sgu(g, pool, gu=None, delay=0, vl=None, altw=False, gv=None):
            for _ in range(delay):
                yield
            GU, GUk = gu if gu is not None else GUstd
            gV, gVk = gv if gv is not None else (gV_std, gVk_std)
            vln4, vlk = vl if vl is not None else (vln4_std, ["qb%d" % i for i in range(4)])
            Hb, Hbk = Hb2[g % 2], Hbk2[g % 2]
            CTX("mix_load", pool)
            DMA("sp", Hb[:], hsb[g].rearrange("p (c t) -> p c t", t=GS), "LHB_%d" % g, ["hsb%d" % g], Hbk)
            yield
            CTX("sgu", pool)
            if altw:
                for hh in range(2):
                    DMA("pool", WG[hh][:], win[4][:, :, hh * 256:(hh + 1) * 256], "WG%d" % hh, [], ["WG%d" % hh])
            else:
                wu_, wuk = load_wb(win[4])
            for gg in range(4):
                CTX("sgu", pool)
                pu, puk = bank()
                for k in range(KC):
                    if altw:
                        lw, lwk = WG[gg // 2][:, k, (gg % 2) * P:(gg % 2 + 1) * P], "WG%d" % (gg // 2)
                    else:
                        lw, lwk = wu_[:, k, gg * P:(gg + 1) * P], wuk
                    MM(pu[:, :], lw, Hb[:, k, :], k == 0, k == KC - 1, [lwk, Hbk[k]], [puk])
                ACT(GU[:, gg, :], pu[:, :], AF.Gelu_apprx_tanh, [puk], [GUk[gg]])
                yield
            CTX("sgu", pool)
            if altw:
                for hh in range(2):
                    DMA("pool", WU[hh][:], win[5][:, :, hh * 256:(hh + 1) * 256], "WU%d" % hh, [], ["WU%d" % hh])
            else:
                wv_, wvk = load_wb(win[5])
            for i in range(4):
                CTX("sgu", pool)
                ts_ = slice(i * P, (i + 1) * P)
                pv, pvk = bank()
                if altw:
                    for hh in range(2):
                        for k in range(KC):
                            MM(pv[:, hh * 256:(hh + 1) * 256], Hb[:, k, ts_], WU[hh][:, k, :], k == 0, k == KC - 1,
                               [Hbk[k], "WU%d" % hh], [pvk])
                else:
                    for k in range(KC):
                        MM(pv[:, :], Hb[:, k, ts_], wv_[:, k, :], k == 0, k == KC - 1, [Hbk[k], wvk], [pvk])
                ACT(gV[i], pv[:, :], AF.Gelu_apprx_tanh, [pvk], [gVk[i]])
                for gg in range(4):
                    S.op("dve", (lambda i, gg: lambda e: e.bn_stats(out=bst16[:, i * 4 + gg, :], in_=gV[i][:, gg * P:(gg + 1) * P]))(i, gg),
                         reads=[gVk[i]], writes=["bst16"])
                for gg in range(4):
                    S.op("dve", (lambda i, gg: lambda e: e.bn_aggr(out=bmv16[:, i * 4 + gg, :], in_=bst16[:, i * 4 + gg, :]))(i, gg),
                         reads=["bst16"], writes=["bmv16"])
                yield
            CTX("sgu", pool)
            ACT(r16[:], bmv16[:, :, 1], AF.Ln, ["bmv16", "epsc"], ["r16"], bias=EPS_AP[:, 0:1])
            ACT(r16[:], r16[:], AF.Exp, ["r16"], ["r16"], scale=-0.5)
            STT(n16[:], bmv16[:, :, 0], -1.0, r16[:], ALU.mult, ALU.mult, ["bmv16", "r16"], ["n16"])
            for i in range(4):
                CTX("sgu", pool)
                for gg in range(4):
                    j_ = i * 4 + gg
                    ACT(gV[i][:, gg * P:(gg + 1) * P], gV[i][:, gg * P:(gg + 1) * P], AF.Identity, [gVk[i], "r16", "n16"], [gVk[i]],
                        scale=r16[:, j_:j_ + 1], bias=n16[:, j_:j_ + 1])
                TT(gV[i], gV[i], SGG[:], ALU.mult, [gVk[i], "SGG"], [gVk[i]])
                TT(vln4[:, i, :], gV[i], SGB[:], ALU.add, [gVk[i], "SGB"], [vlk[i]])
                yield
            for i in range(4):
                CTX("sgu", pool)
                ts_ = slice(i * P, (i + 1) * P)
                ps_, psk = bank()
                for gg in range(4):
                    gs_ = slice(gg * P, (gg + 1) * P)
                    MM(ps_[:, gs_], vln4[:, i, gs_], WsT[:, gg, :], True, False, [vlk[i], "WsT"], [psk])
                    MM(ps_[:, gs_], ones_b[0:1, :], bs_hi[0:1, gs_], False, False, ["ones_b", "bs_hi"], [psk])
                    MM(ps_[:, gs_], ones_b[0:1, :], bs_lo[0:1, gs_], False, False, ["ones_b", "bs_lo"], [psk])
                    MM(ps_[:, gs_], ones_b[0:1, :], bs_lo2[0:1, gs_], False, True, ["ones_b", "bs_lo2"], [psk])
                TT(mix[:, 4:8, ts_], ps_[:, :].rearrange("p (g t) -> p g t", t=P), GU[:, :, ts_], ALU.mult,
                   [psk] + list(GUk), mixk[4:8])
                yield

        def gen_mixer_fin(g, pool, gu=None):
            GU, GUk = gu if gu is not None else GUstd
            Hb, Hbk = Hb2[g % 2], Hbk2[g % 2]
            CTX("hg_fin", pool)
            DMA("sp", OL[:], osp[g].rearrange("p (h t) -> p h t", t=GS), "LOL_%d" % g, ["osp%d" % g], ["OL%d" % i for i in range(4)])
            DMA("sp", QP[:], qsp[g].rearrange("p (h t) -> p h t", t=GS), "LQP_%d" % g, ["qsp%d" % g], ["QP%d" % h for h in range(4)])
            wg_, wgk = load_wb(win[3])
            for h in range(4):
                CTX("hg_fin", pool)
                pc, pck = bank()
                MM(pc[:, :], Spb[:, h, :], QP[:, h, :], True, True, ["Spb", "QP%d" % h], [pck])
                TT(gV[h], OL[:, h, :], pc[:, :], ALU.add, ["OL%d" % i for i in range(4)] + [pck], [gVk[h]])
                ACT(sqs[h][0], gV[h], AF.Square, [gVk[h]], [sqs[h][1]])
                yield
            for h in range(4):
                CTX("hg_fin", pool)
                pm, pmk = bank()
                MM(pm[:, :], ones128[:], sqs[h][0], True, True, ["ones128", sqs[h][1]], [pmk])
                ACT(sqs[h][0], pm[:, :], AF.Ln, [pmk, "epsc"], [sqs[h][1]], bias=EPS_AP[:, 0:1])
                yield
            for h in range(4):
                CTX("hg_fin", pool)
                ACT(sqs[h][0], sqs[h][0], AF.Exp, [sqs[h][1]], [sqs[h][1]], scale=-0.5)
                TT(gV[h], gV[h], sqs[h][0], ALU.mult, [gVk[h], sqs[h][1]], [gVk[h]])
            yield
            for h in range(4):
                CTX("hg_fin", pool)
                hs = slice(h * P, (h + 1) * P)
                pg, pgk = bank()
                for k in range(KC):
                    MM(pg[:, :], wg_[:, k, hs], Hb[:, k, :], k == 0, k == KC - 1, [wgk, Hbk[k]], [pgk])
                ACT(GU[:, h, :], pg[:, :], AF.Silu, [pgk], [GUk[h]])
                STT(mix[:, h, :], gV[h], colt[:, C_HGN:C_HGN + 1], GU[:, h, :], ALU.mult, ALU.mult,
                    [gVk[h], GUk[h], "colt"], [mixk[h]])
                yield

        def gen_rest(g, pool):
            gsl = slice(g * GS, (g + 1) * GS)
            Hb, Hbk = Hb2[g % 2], Hbk2[g % 2]
            CTX("w_out", pool)
            DMA("sp", H[:], hsp[g].rearrange("p (c t) -> p c t", t=GS), "LHs_%d" % g, ["hsp%d" % g], Hk)
            if stage == 20:
                ACT(H[:], mix[:], AF.Copy, mixk, Hk)
                DMA("sp", outT[:, :, gsl], H[:], "OUT_%d" % g, Hk, ["out%d" % g])
                return
            wo0, wo0k = load_wb(wout[0])
            wo1, wo1k = load_wb(wout[1])
            for c in range(KC):
                wt, wtk = (wo0, wo0k) if c < 4 else (wo1, wo1k)
                cs = slice((c % 4) * P, (c % 4 + 1) * P)
                po, pok = bank()
                for k in range(KC):
                    MM(po[:, :], wt[:, k, cs], mix[:, k, :], k == 0, k == KC - 1, [wtk, mixk[k]], [pok])
                TT(H[:, c, :], po[:, :], H[:, c, :], ALU.add, [pok, Hk[c]], [Hk[c]])
            yield
            if stage == 2:
                yield from layer_norm(H, Hk, GS, C_LN[2][0], C_LN[2][1], "ln", pool, out_f=H, out_fk=Hk)
                DMA("sp", outT[:, :, gsl], H[:], "OUT_%d" % g, Hk, ["out%d" % g])
                return
            yield from layer_norm(H, Hk, GS, C_LN[2][0], C_LN[2][1], "ln", pool, out_f=H, out_fk=Hk,
                                  agcol=C_LN[2][0], abcol=C_LN[2][1], out_b=Hb, out_bk=Hbk)
            CTX("xattn", pool)
            wq0, wq0k = load_wb(wq[0])
            wq1, wq1k = load_wb(wq[1])
            for fc_ in range(KC):
                wt, wtk = (wq0, wq0k) if fc_ < 4 else (wq1, wq1k)
                cs = slice((fc_ % 4) * P, (fc_ % 4 + 1) * P)
                pq, pqk = bank()
                for k in range(KC):
                    MM(pq[:, :], wt[:, k, cs], Hb[:, k, :], k == 0, k == KC - 1, [wtk, Hbk[k]], [pqk])
                ACT(QT[:, fc_, :], pq[:, :], AF.Copy, [pqk], ["QT%d" % fc_])
            yield
            CTX("xattn", pool)
            for h in range(4):
                for mt in range(2):
                    psc, psck = bank()
                    for ec in range(2):
                        MM(psc[:, :], KT[:, h * 2 + ec, mt * P:(mt + 1) * P], QT[:, h * 2 + ec, :], ec == 0, ec == 1,
                           ["KT", "QT%d" % (h * 2 + ec)], [psck])
                    ACT(PT[mt], psc[:, :], AF.Exp, [psck], ["PT%d" % mt], scale=1.0 / 16.0)
                psm, psmk = bank()
                for mt in range(2):
                    MM(psm[:, :], ones_b[:], PT[mt], mt == 0, mt == 1, ["ones_b", "PT%d" % mt], [psmk])
                ACT(tA, psm[:, :], AF.Ln, [psmk], ["tA"])
                ACT(tA, tA, AF.Exp, ["tA"], ["tA"], scale=-1.0)
                for ec in range(2):
                    pv, pvk = bank()
                    for mt in range(2):
                        MM(pv[:, :], Vm[:, mt, (h * 2 + ec) * P:(h * 2 + ec + 1) * P], PT[mt], mt == 0, mt == 1,
                           ["Vm", "PT%d" % mt], [pvk])
                    TT(mix[:, h * 2 + ec, :], pv[:, :], tA, ALU.mult, [pvk, "tA"], [mixk[h * 2 + ec]])
            yield
            CTX("xattn", pool)
            wo0, wo0k = load_wb(wo[0])
            wo1, wo1k = load_wb(wo[1])
            for c in range(KC):
                wt, wtk = (wo0, wo0k) if c < 4 else (wo1, wo1k)
                cs = slice((c % 4) * P, (c % 4 + 1) * P)
                po, pok = bank()
                for k in range(KC):
                    MM(po[:, :], wt[:, k, cs], mix[:, k, :], k == 0, k == KC - 1, [wtk, mixk[k]], [pok])
                TT(H[:, c, :], po[:, :], H[:, c, :], ALU.add, [pok, Hk[c]], [Hk[c]])
            yield
            if stage == 3:
                yield from layer_norm(H, Hk, GS, C_LN[3][0], C_LN[3][1], "ln", pool, out_f=H, out_fk=Hk)
                DMA("sp", outT[:, :, gsl], H[:], "OUT_%d" % g, Hk, ["out%d" % g])
                return
            yield from layer_norm(H, Hk, GS, C_LN[3][0], C_LN[3][1], "ln", pool, out_f=H, out_fk=Hk,
                                  agcol=C_LN[3][0], abcol=C_LN[3][1], out_b=Hb, out_bk=Hbk)

        def gen_ffn2(g, pool):
            gsl = slice(g * GS, (g + 1) * GS)
            Hb, Hbk = Hb2[g % 2], Hbk2[g % 2]
            yield from ffn(w2g, w2u, w2d, Hb, Hbk, pool, "w2", g == 0)
            yield from layer_norm(H, Hk, GS, C_LN[4][0], C_LN[4][1], "ln", pool, out_f=H, out_fk=Hk)
            CTX("store", pool)
            DMA("sp", outT[:, :, gsl], H[:], "OUT_%d" % g, Hk, ["out%d" % g])
            yield

        interleave(chain(gen_gu1(0, "A"), gen_down1(0, "A")), 32, gen_setup_kv("B"), 15)
        if stage >= 2:
            for g in range(NG):
                A = chain(gen_ln1(g, "B"), gen_hg(g, "B"))
                if g + 1 < NG:
                    interleave(A, 52, chain(gen_gu1(g + 1, "A"), gen_down1(g + 1, "A")), 32)
                else:
                    interleave(A, 52, gen_mixer_sgu(0, "A", (GUalt, GUaltk), delay=36, vl=(vln4_alt, ["scr%d" % (13 + i) for i in range(4)]),
                                                    altw=True, gv=([H[:, c_, :] for c_ in range(4)], Hk[0:4])), 44)
        else:
            run(gen_ln1(0, "all"))
            for g in range(1, NG):
                run(chain(gen_gu1(g, "all"), gen_down1(g, "all"), gen_ln1(g, "all")))

        if stage >= 2:
            CTX("xchg", "all")
            DMA("pool", cin.ap().rearrange("(h d) v -> d h v", d=P), Sst[:], "XC", ["S0", "S1", "S2", "S3"], ["cin"])
            S.dma("pool", lambda e: e.collective_compute("AllGather", ALU.bypass, replica_groups=[[0, 1], [2, 3], [4, 5], [6, 7]],
                                                         ins=[cin.ap().opt()], outs=[cout.ap().opt()]),
                  "CC", reads=["cin"], writes=["cout"], inc=1)
            DMA("pool", Sp[:], cout.ap()[0:4 * P, :].rearrange("(h d) v -> d h v", d=P), "XC", ["cout"], ["Sp"])
            TS(Spb[:], Sp[:], flg[:, 0:1], None, ALU.mult, None, ["Sp", "flg"], ["Spb"])

            if stage >= 4:
                run(gen_mixer_fin(0, "all", (GUalt, GUaltk)))
            else:
                run(gen_mixer(0, "all"))
            for g in range(NG):
                run(gen_rest(g, "all"))
                if stage in (2, 3, 20):
                    if g + 1 < NG:
                        run(gen_mixer(g + 1, "all"))
                    continue
                if g + 1 < NG:
                    interleave(gen_ffn2(g, "A"), 33, gen_mixer(g + 1, "B"), 31)
                else:
                    run(gen_ffn2(g, "all"))

        S.wait_all("sp", ["out%d" % g for g in range(NG)])
        S.wait_all("act", ["out%d" % g for g in range(NG)])
        build_program.last_sched = S
        print("SBUF bytes/partition:", budget[0], "instr:", {e: len(v) for e, v in S.streams.items()})
        with nc.Block() as block:
            S.replay(block)
    return nc


def _blk_kn(w, nb):
    K, N = w.shape
    return np.ascontiguousarray(w.reshape(K // P, P, N // nb, nb).transpose(2, 1, 0, 3))


def prepare_inputs(x, mem, ffn1_w_gate, ffn1_w_up, ffn1_w_down, ln1_g, ln1_b, w_in, hg_lb_logits, hg_norm_g,
                   sg_ln_g, sg_ln_b, sg_w_s, sg_b_s, w_out, ln2_g, ln2_b, mem_ln_g, mem_ln_b, xa_w_q, xa_w_k,
                   xa_w_v, xa_w_o, ln3_g, ln3_b, ffn2_w_gate, ffn2_w_up, ffn2_w_down, ln4_g, ln4_b):
    f = lambda a: np.asarray(a, dtype=np.float32)
    col = lambda v: f(v).reshape(KC, P).T
    cols = np.zeros((P, NCOLS), np.float32)
    for idx, (g_, b_) in ((1, (ln1_g, ln1_b)), (2, (ln2_g, ln2_b)), (3, (ln3_g, ln3_b)), (4, (ln4_g, ln4_b)),
                          ("m", (mem_ln_g, mem_ln_b))):
        cols[:, C_LN[idx][0]:C_LN[idx][0] + 8] = col(g_[0])
        cols[:, C_LN[idx][1]:C_LN[idx][1] + 8] = col(b_[0])
    lg = f(hg_lb_logits)
    cols[:, C_LB0:C_LB0 + 4] = lg[0].T
    cols[:, C_LB1:C_LB1 + 4] = lg[1].T
    cols[:, C_HGN] = f(hg_norm_g)[0]
    sgrow = np.stack([f(sg_ln_g)[0].reshape(512), f(sg_ln_b)[0].reshape(512), f(sg_b_s)[0].reshape(512)])
    sgw = np.ascontiguousarray(f(sg_w_s)[0].transpose(1, 0, 2))
    shared = {
        "cols": cols, "sgrow": np.ascontiguousarray(sgrow), "sgw": sgw,
        "w1g": _blk_kn(f(ffn1_w_gate)[0], 256), "w1u": _blk_kn(f(ffn1_w_up)[0], 256),
        "w1d": _blk_kn(f(ffn1_w_down)[0], P),
        "w2g": _blk_kn(f(ffn2_w_gate)[0], 256), "w2u": _blk_kn(f(ffn2_w_up)[0], 256),
        "w2d": _blk_kn(f(ffn2_w_down)[0], P),
        "win": _blk_kn(f(w_in)[0], 512), "wout": _blk_kn(f(w_out)[0], 512),
        "wq": _blk_kn(f(xa_w_q)[0], 512), "wk": _blk_kn(f(xa_w_k)[0], 512),
        "wv": _blk_kn(f(xa_w_v)[0], 512), "wo": _blk_kn(f(xa_w_o)[0], 512),
    }
    x = f(x)
    mem = f(mem)
    in_maps = []
    for c in range(NCORES):
        b, half = c // 2, c % 2
        xs = x[b, half * T:(half + 1) * T, :]
        m = dict(shared)
        m["xT"] = np.ascontiguousarray(xs.reshape(T, KC, P).transpose(2, 1, 0))
        m["memT"] = np.ascontiguousarray(mem[b].reshape(MEM, KC, P).transpose(2, 1, 0))
        m["flag"] = np.full((P, 1), float(half), np.float32)
        in_maps.append(m)
    return in_maps


def assemble(results):
    out = np.zeros((4, 2 * T, D), np.float32)
    for c in range(NCORES):
        b, half = c // 2, c % 2
        o = np.asarray(results[c]["outT"])
        out[b, half * T:(half + 1) * T, :] = o.transpose(2, 1, 0).reshape(T, D)
    return out


def kernel(**inputs):
    in_maps = prepare_inputs(**inputs)
    nc = build_program(STAGE)
    res = run_bass_kernel_spmd(nc, in_maps, core_ids=list(range(NCORES)))
    return assemble(res.results)
```
